# Optimizing a Trainium2 kernel written in Bass

```python
import jax, jax.numpy as jnp
from jax import lax
import numpy as np

D_MODEL = 1024
BATCH = 8
SEQ = 4096
DEPTH = 1

CHUNK = 64
N_META = 16
Q_BLOCK = 128
N_PAD = Q_BLOCK - N_META
PREFIX = N_PAD + N_META
MIX_WIDTH = D_MODEL
RET_HEADS = 4
RET_DV = (MIX_WIDTH // 2) // RET_HEADS
RET_DK = RET_DV // 2
FOX_HEADS = 8
FOX_DH = (MIX_WIDTH // 2) // FOX_HEADS
D_FF = 2816
CONV_W = 3
ROPE_BASE = 10000.0
EPS = 1e-6
NEG = -1e30

RET_QK = RET_HEADS * RET_DK
RET_V = RET_HEADS * RET_DV
FOX_W = FOX_HEADS * FOX_DH
SPLIT_POINTS = (RET_QK, 2 * RET_QK, 2 * RET_QK + RET_V, 2 * RET_QK + 2 * RET_V,
                2 * RET_QK + 2 * RET_V + FOX_W, 2 * RET_QK + 2 * RET_V + 2 * FOX_W,
                2 * RET_QK + 2 * RET_V + 3 * FOX_W)
IN_WIDTH = 2 * RET_QK + 2 * RET_V + 3 * FOX_W + FOX_HEADS

kernel_name = "hymba_retention_fox_convffn_block"


def rms_norm(x, g):
    xf = x.astype(jnp.float32)
    y = xf * lax.rsqrt(jnp.mean(xf * xf, axis=-1, keepdims=True) + EPS)
    return (y * g.astype(jnp.float32)).astype(x.dtype)


def rotary(x, pos):
    half = x.shape[-1] // 2
    inv = 1.0 / (ROPE_BASE ** (jnp.arange(half, dtype=jnp.float32) / half))
    ang = pos.astype(jnp.float32)[:, None] * inv[None, :]
    cos = jnp.cos(ang)[None, :, None, :]
    sin = jnp.sin(ang)[None, :, None, :]
    x1 = x[..., :half].astype(jnp.float32)
    x2 = x[..., half:].astype(jnp.float32)
    return jnp.concatenate([x1 * cos - x2 * sin, x1 * sin + x2 * cos], axis=-1)


def retention(q, k, v, valid):
    B, L, H, dk = q.shape
    dv = v.shape[-1]
    nc = L // CHUNK
    f32 = jnp.float32
    log_g = jnp.log1p(-jnp.exp2(-5.0 - jnp.arange(H, dtype=f32)))
    pos = jnp.arange(L)
    qr = rotary(q, pos)
    kr = rotary(k, pos) * (dk ** -0.5) * valid.astype(f32)[None, :, None, None]
    qc = qr.reshape(B, nc, CHUNK, H, dk)
    kc = kr.reshape(B, nc, CHUNK, H, dk)
    vc = v.astype(f32).reshape(B, nc, CHUNK, H, dv)
    n = jnp.arange(CHUNK, dtype=f32)
    d_intra = jnp.exp(jnp.abs(n[:, None] - n[None, :])[None] * log_g[:, None, None])
    s = jnp.einsum('bcnhd,bcmhd->bchnm', qc, kc) * d_intra[None, None]
    intra = jnp.einsum('bchnm,bcmhe->bcnhe', s, vc)
    w_k = jnp.exp((CHUNK - 1.0 - n)[:, None] * log_g[None, :])
    u = jnp.einsum('bcmhd,bcmhe->bchde', kc * w_k[None, None, :, :, None], vc)
    g_chunk = jnp.exp(CHUNK * log_g)[None, :, None, None]

    def step(r, u_i):
        return g_chunk * r + u_i, r

    _, r_prev = lax.scan(step, jnp.zeros((B, H, dk, dv), f32), jnp.moveaxis(u, 1, 0))
    r_prev = jnp.moveaxis(r_prev, 0, 1)
    w_q = jnp.exp((n + 1.0)[:, None] * log_g[None, :])
    inter = jnp.einsum('bcnhd,bchde->bcnhe', qc * w_q[None, None, :, :, None], r_prev)
    return (intra + inter).reshape(B, L, H, dv)


def forgetting_attention(q, k, v, log_f, valid):
    B, L, H, dh = q.shape
    f32 = jnp.float32
    scale = dh ** -0.5
    c = jnp.cumsum(log_f.astype(f32), axis=1).transpose(0, 2, 1)
    qf, kf, vf = q.astype(f32), k.astype(f32), v.astype(f32)
    pos = jnp.arange(L)
    outs = []
    for blk in range(L // Q_BLOCK):
        q0, q1 = blk * Q_BLOCK, (blk + 1) * Q_BLOCK
        logits = jnp.einsum('bqhd,bkhd->bhqk', qf[:, q0:q1], kf[:, :q1]) * scale
        bias = c[:, :, q0:q1, None] - c[:, :, None, :q1]
        mask = (pos[None, :q1] <= pos[q0:q1, None]) & valid[None, :q1]
        logits = jnp.where(mask[None, None], logits + bias, NEG)
        p = jax.nn.softmax(logits, axis=-1)
        outs.append(jnp.einsum('bhqk,bkhd->bqhd', p, vf[:, :q1]))
    return jnp.concatenate(outs, axis=1)


def hybrid_mixer(h, w_in, forget_b, ret_norm_g, w_out, valid):
    B, L, _ = h.shape
    proj = jnp.einsum('bld,de->ble', h, w_in)
    rq, rk, rv, rg, fq, fk, fv, ff = jnp.split(proj, SPLIT_POINTS, axis=-1)
    o_r = retention(rq.reshape(B, L, RET_HEADS, RET_DK), rk.reshape(B, L, RET_HEADS, RET_DK),
                    rv.reshape(B, L, RET_HEADS, RET_DV), valid)
    o_r = o_r * lax.rsqrt(jnp.mean(o_r * o_r, axis=-1, keepdims=True) + EPS)
    o_r = o_r.reshape(B, L, RET_V) * ret_norm_g.astype(jnp.float32) * jax.nn.silu(rg.astype(jnp.float32))
    log_f = jax.nn.log_sigmoid(ff.astype(jnp.float32) + forget_b.astype(jnp.float32))
    o_f = forgetting_attention(fq.reshape(B, L, FOX_HEADS, FOX_DH), fk.reshape(B, L, FOX_HEADS, FOX_DH),
                               fv.reshape(B, L, FOX_HEADS, FOX_DH), log_f, valid).reshape(B, L, FOX_W)
    mixed = jnp.concatenate([o_r, o_f], axis=-1).astype(h.dtype)
    return jnp.einsum('ble,ed->bld', mixed, w_out)


def conv_ffn(h, w_up, conv_w, conv_b, w_down, valid):
    L = h.shape[1]
    up = jnp.einsum('bld,df->blf', h, w_up)
    a, b = jnp.split(up, 2, axis=-1)
    a = a * valid.astype(a.dtype)[None, :, None]
    a_pad = jnp.pad(a, ((0, 0), (CONV_W - 1, 0), (0, 0)))
    acc = conv_b
    for j in range(CONV_W):
        acc = acc + a_pad[:, j:j + L] * conv_w[j]
    return jnp.einsum('blf,fd->bld', jax.nn.silu(acc) * b, w_down)


def setup_inputs(seed: int = 0) -> dict:
    key = jax.random.key(seed)
    ks = jax.random.split(key, 13)
    f32 = jnp.float32
    return {
        "x": jax.random.normal(ks[0], (BATCH, SEQ, D_MODEL), f32),
        "meta_tokens": jax.random.normal(ks[1], (N_META, D_MODEL), f32),
        "attn_norm_g": 1.0 + 0.02 * jax.random.normal(ks[2], (DEPTH, D_MODEL), f32),
        "w_in": jax.random.normal(ks[3], (DEPTH, D_MODEL, IN_WIDTH), f32) * D_MODEL ** -0.5,
        "fox_forget_b": jax.random.uniform(ks[4], (DEPTH, FOX_HEADS), f32, 1.0, 5.0),
        "ret_norm_g": 1.0 + 0.02 * jax.random.normal(ks[5], (DEPTH, RET_V), f32),
        "w_out": jax.random.normal(ks[6], (DEPTH, MIX_WIDTH, D_MODEL), f32) * MIX_WIDTH ** -0.5,
        "ffn_norm_g": 1.0 + 0.02 * jax.random.normal(ks[7], (DEPTH, D_MODEL), f32),
        "w_up": jax.random.normal(ks[8], (DEPTH, D_MODEL, 2 * D_FF), f32) * D_MODEL ** -0.5,
        "conv_w": jax.random.normal(ks[9], (DEPTH, CONV_W, D_FF), f32) * CONV_W ** -0.5,
        "conv_b": 0.02 * jax.random.normal(ks[10], (DEPTH, D_FF), f32),
        "w_down": jax.random.normal(ks[11], (DEPTH, D_FF, D_MODEL), f32) * D_FF ** -0.5,
        "final_norm_g": 1.0 + 0.02 * jax.random.normal(ks[12], (D_MODEL,), f32),
    }


def reference(x, meta_tokens, attn_norm_g, w_in, fox_forget_b, ret_norm_g, w_out,
              ffn_norm_g, w_up, conv_w, conv_b, w_down, final_norm_g):
    B = x.shape[0]
    pad = jnp.zeros((B, N_PAD, D_MODEL), x.dtype)
    meta = jnp.broadcast_to(meta_tokens.astype(x.dtype)[None], (B, N_META, D_MODEL))
    h = jnp.concatenate([pad, meta, x], axis=1)
    valid = jnp.arange(h.shape[1]) >= N_PAD
    for layer in range(DEPTH):
        h = h + hybrid_mixer(rms_norm(h, attn_norm_g[layer]), w_in[layer], fox_forget_b[layer],
                             ret_norm_g[layer], w_out[layer], valid)
        h = h + conv_ffn(rms_norm(h, ffn_norm_g[layer]), w_up[layer], conv_w[layer], conv_b[layer],
                         w_down[layer], valid)
    return rms_norm(h, final_norm_g)[:, PREFIX:]
```

```python
import numpy as np
from contextlib import ExitStack
import concourse.bass as bass
import concourse.mybir as mybir
from concourse.bass_utils import run_bass_kernel_spmd

F32 = mybir.dt.float32
BF16 = mybir.dt.bfloat16
AF = mybir.ActivationFunctionType
ALU = mybir.AluOpType

D = 1024
NPAD = 112
NMETA = 16
RET_H = 4
FOX_H = 8
DFF = 2816
NFC = DFF // 128
INW = 3080
EPS = 1e-6
NCORES = 8


class _Op:
    __slots__ = ("eng", "fn", "deps", "sig", "sigval", "dma", "dsem", "dval", "eidx", "dn")

    def __init__(self, eng, fn, dma):
        self.eng = eng
        self.fn = fn
        self.deps = ()
        self.sig = False
        self.sigval = 0
        self.dma = dma
        self.dsem = None
        self.dval = 0
        self.eidx = 0
        self.dn = 0


class Prog:
    COMPUTE = ("pe", "act", "dve", "pool")
    QUEUES = ("sp", "pool", "act")
    NPOOL = 12
    WINDOW = 6

    def __init__(self, nc, es):
        self.nc = nc
        self.engs = {"pe": nc.tensor, "act": nc.scalar, "dve": nc.vector, "pool": nc.gpsimd, "sp": nc.sync}
        self.ops = []
        self.eng_ops = {e: [] for e in self.engs}
        self.last_writer = {}
        self.readers = {}
        self.sems = {e: es.enter_context(nc.semaphore("s_" + e)) for e in self.COMPUTE}
        self.dma_pools = {q: [es.enter_context(nc.semaphore("d_%s%d" % (q, k))) for k in range(self.NPOOL)]
                          for q in self.QUEUES}
        self.dma_count = {q: 0 for q in self.QUEUES}
        self.barrier_deps = {e: None for e in self.engs}
        self.all_dma = []
        self.stopped = False

    def add(self, eng, fn, reads=(), writes=(), dma=False, force=False):
        if self.stopped and not force:
            return None
        op = _Op(eng, fn, dma)
        idx = len(self.ops)
        lst = self.eng_ops[eng]
        op.eidx = len(lst)
        deps = set()
        bd = self.barrier_deps[eng]
        if bd is not None:
            deps.update(bd)
            self.barrier_deps[eng] = None
        for r in reads:
            w = self.last_writer.get(r)
            if w is not None:
                deps.add(w)
            if isinstance(r, str) and r.startswith("ps:"):
                for o in self.readers.get(r, ()):
                    if self.ops[o].eng != eng:
                        deps.add(o)
        for w_ in writes:
            w = self.last_writer.get(w_)
            if w is not None:
                deps.add(w)
            rs = self.readers.get(w_)
            if rs:
                deps.update(rs)
        keep = []
        for d in deps:
            p = self.ops[d]
            if p.dma:
                keep.append(d)
            elif p.eng != eng:
                p.sig = True
                keep.append(d)
            else:
                if eng == "pe":
                    continue
                if op.eidx - p.eidx <= self.WINDOW:
                    p.sig = True
                    keep.append(d)
        op.deps = keep
        for w_ in writes:
            self.last_writer[w_] = idx
            self.readers[w_] = []
        for r in reads:
            self.readers.setdefault(r, []).append(idx)
        if dma:
            op.dn = self.dma_count[eng]
            self.dma_count[eng] += 1
            self.all_dma.append(idx)
        self.ops.append(op)
        lst.append(idx)
        return idx

    def barrier(self):
        deps = []
        for e in self.COMPUTE:
            for idx in reversed(self.eng_ops[e]):
                if not self.ops[idx].dma:
                    self.ops[idx].sig = True
                    deps.append(idx)
                    break
        deps.extend(self.all_dma)
        for e in self.engs:
            self.barrier_deps[e] = list(deps)
        self.last_writer = {}
        self.readers = {}

    def finish(self):
        self.barrier()
        self.add("sp", None, force=True)

    def emit(self, block):
        cnt = {e: 0 for e in self.COMPUTE}
        for op in self.ops:
            if op.dma:
                pool = self.dma_pools[op.eng]
                op.dsem = pool[op.dn % self.NPOOL]
                op.dval = 16 * (op.dn // self.NPOOL + 1)
            elif op.sig:
                cnt[op.eng] += 1
                op.sigval = cnt[op.eng]
        ops = self.ops
        sems = self.sems
        prog = self

        def run_engine(ename, eng):
            waited = {}
            for idx in prog.eng_ops[ename]:
                op = ops[idx]
                waits = {}
                for d in op.deps:
                    p = ops[d]
                    if p.dma:
                        s, v = p.dsem, p.dval
                    else:
                        s, v = sems[p.eng], p.sigval
                    k = id(s)
                    if k not in waits or waits[k][1] < v:
                        waits[k] = (s, v)
                if op.dma and op.dn >= prog.NPOOL:
                    s, v = op.dsem, op.dval - 16
                    k = id(s)
                    if k not in waits or waits[k][1] < v:
                        waits[k] = (s, v)
                for k, (s, v) in waits.items():
                    if waited.get(k, 0) < v:
                        eng.wait_ge(s, v)
                        waited[k] = v
                if op.fn is None:
                    continue
                inst = op.fn(eng)
                if op.dma:
                    inst.then_inc(op.dsem, 16)
                elif op.sig:
                    inst.then_inc(sems[ename], 1)

        @block.tensor
        def _(e):
            run_engine("pe", e)

        @block.scalar
        def _(e):
            run_engine("act", e)

        @block.vector
        def _(e):
            run_engine("dve", e)

        @block.gpsimd
        def _(e):
            run_engine("pool", e)

        @block.sync
        def _(e):
            run_engine("sp", e)


class _Stop(Exception):
    pass


def build_program(NT=33, debug=False, stop=None):
    assert (NT - 1) % 4 == 0
    L = NT * 128
    NS = (NT - 1) // 4
    nc = bass.Bass("TRN2", target_bir_lowering=False)

    def din(name, shape, dt=F32):
        return nc.dram_tensor(name, list(shape), dt, kind="ExternalInput").ap()

    xe = din("xe", [L, D])
    w_in_d = din("w_in", [D, INW])
    w_out_d = din("w_out", [D, D])
    w_up_d = din("w_up", [D, 2 * DFF])
    w_down_d = din("w_down", [DFF, D])
    g_attn_d = din("g_attn_c", [128, D])
    g_ffn_d = din("g_ffn_c", [128, D])
    g_fin_d = din("g_fin_c", [128, D])
    g_ret_d = din("g_ret_c", [128, 512])
    fb_d = din("fb_c", [128, 8])
    cw_d = din("cw_c", [128, NFC, 3])
    cb_d = din("cb_c", [128, NFC])
    cos_d = din("cos_c", [128, NT, 32])
    sin_d = din("sin_c", [128, NT, 32])
    d2_d = din("d2", [128, 4, 128])
    wq2_d = din("wq2", [128, 2, 128])
    wk_d = din("wk_c", [128, 4])
    mask_d = din("mask_c", [128, 2, 128])
    ident_d = din("ident", [128, 128])
    utri_d = din("utri", [128, 128])
    ones_d = din("ones_c", [128, 128])
    padneg_d = din("padneg", [128, 8])
    out_d = nc.dram_tensor("out", [L - 128, D], F32, kind="ExternalOutput").ap()
    h2_d = nc.dram_tensor("h2s", [L, D], F32, kind=("ExternalOutput" if debug else "Internal")).ap()

    GD = [float((1.0 - 2.0 ** (-5.0 - h)) ** 128) for h in range(RET_H)]

    with ExitStack() as es:
        P = Prog(nc, es)

        def chk(k):
            if stop is not None and stop == k:
                P.stopped = True

        def mm(out, lhsT, rhs, start, stop, reads, writes):
            P.add("pe", lambda e: e.matmul(out, lhsT=lhsT, rhs=rhs, start=start, stop=stop), reads, writes)

        def tr(out, in_, ident, reads, writes):
            P.add("pe", lambda e: e.transpose(out=out, in_=in_, identity=ident), reads, writes)

        def act(out, in_, func, reads, writes, bias=None, scale=None, accum=None):
            kw = {}
            if bias is not None:
                kw["bias"] = bias
            if scale is not None:
                kw["scale"] = scale
            if accum is not None:
                kw["accum_out"] = accum
            P.add("act", lambda e: e.activation(out=out, in_=in_, func=func, **kw), reads, writes)

        def tt(eng, out, in0, in1, op, reads, writes):
            P.add(eng, lambda e: e.tensor_tensor(out=out, in0=in0, in1=in1, op=op), reads, writes)

        def ts(eng, out, in0, s1, s2, op0, op1, reads, writes):
            if op1 is None:
                P.add(eng, lambda e: e.tensor_scalar(out=out, in0=in0, scalar1=s1, scalar2=None, op0=op0), reads, writes)
            else:
                P.add(eng, lambda e: e.tensor_scalar(out=out, in0=in0, scalar1=s1, scalar2=s2, op0=op0, op1=op1),
                      reads, writes)

        def stt(out, in0, scalar, in1, op0, op1, reads, writes):
            P.add("dve", lambda e: e.scalar_tensor_tensor(out=out, in0=in0, scalar=scalar, in1=in1, op0=op0, op1=op1),
                  reads, writes)

        def cp(eng, out, in_, reads, writes):
            if eng == "act":
                P.add("act", lambda e: e.copy(out=out, in_=in_), reads, writes)
            else:
                P.add(eng, lambda e: e.tensor_copy(out=out, in_=in_), reads, writes)

        def recip(out, in_, reads, writes):
            P.add("dve", lambda e: e.reciprocal(out=out, in_=in_), reads, writes)

        def mset(eng, ap, val, writes):
            P.add(eng, lambda e: e.memset(ap, val), (), writes)

        def dma(q, out, in_, reads, writes, **kw):
            P.add(q, lambda e: e.dma_start(out=out, in_=in_, **kw), reads, writes, dma=True)

        Ta = es.enter_context(nc.psum_tensor("Ta", [128, 8, 128], BF16))
        Tb = es.enter_context(nc.psum_tensor("Tb", [128, 8, 128], BF16))
        P0 = es.enter_context(nc.psum_tensor("P0", [128, 512], F32))
        P1 = es.enter_context(nc.psum_tensor("P1", [128, 512], F32))
        R0 = es.enter_context(nc.psum_tensor("R0", [128, 512], F32))
        R1 = es.enter_context(nc.psum_tensor("R1", [128, 512], F32))
        S0 = es.enter_context(nc.psum_tensor("S0", [128, 512], F32))
        S1 = es.enter_context(nc.psum_tensor("S1", [128, 512], F32))

        ident = es.enter_context(nc.sbuf_tensor("ident_sb", [128, 128], BF16))
        dma("pool", ident[:], ident_d, (), ["ident"])

        try:
            with ExitStack() as e1:
                def sb(name, shape, dt):
                    return e1.enter_context(nc.sbuf_tensor(name, list(shape), dt))

                W_in = sb("W_in", [128, 8, INW], BF16)
                W_out = sb("W_out", [128, 8, D], BF16)
                FKT = sb("FKT", [128, 4, L], BF16)
                VA = sb("VA", [128, NT, 8, 65], BF16)
                cos_t = sb("cos_t", [128, NT, 32], F32)
                sin_t = sb("sin_t", [128, NT, 32], F32)
                D2 = sb("D2", [128, 4, 128], F32)
                Wq2 = sb("Wq2", [128, 2, 128], F32)
                wk = sb("wk", [128, 4], F32)
                mask = sb("mask", [128, 2, 128], BF16)
                U = sb("U", [128, 128], F32)
                ones = sb("ones", [128, 128], F32)
                g_attn = sb("g_attn", [128, D], F32)
                g_ret = sb("g_ret", [128, 512], F32)
                fb = sb("fb", [128, 8], F32)
                padneg = sb("padneg_sb", [128, 8], F32)
                xt = [sb("xt%d" % k, [128, D], F32) for k in range(2)]
                hn_bf = sb("hn_bf", [128, D], BF16)
                hnT = sb("hnT", [128, 8, 128], BF16)
                small = sb("small", [128, 64], F32)
                qk = sb("qk", [128, 512], F32)
                t1 = sb("t1", [128, 256], F32)
                t2 = sb("t2", [128, 256], F32)
                qkr = sb("qkr", [128, 512], BF16)
                kw_bf = sb("kw_bf", [128, 256], BF16)
                rv_bf = sb("rv_bf", [128, 512], BF16)
                eg = sb("eg", [128, 512], F32)
                gate = sb("gate", [128, 512], F32)
                fq_bf = sb("fq_bf", [128, 512], BF16)
                fk_bf = sb("fk_bf", [128, 512], BF16)
                rqT = sb("rqT", [128, 2, 128], BF16)
                rqwT = sb("rqwT", [128, 2, 128], BF16)
                rkTz = sb("rkTz", [128, 4, 128], BF16)
                fqT = sb("fqT", [128, 4, 128], BF16)
                STd = sb("STd", [128, 4, 128], BF16)
                R = sb("R", [128, 2, 128], F32)
                R_bfz = sb("R_bfz", [128, 4, 128], BF16)
                NPT = 16
                PT = [sb("PT%d" % k, [128, 128], BF16) for k in range(NPT)]
                junk = sb("junk", [128, 128], BF16)
                mixed = sb("mixed", [128, D], BF16)
                mixedT = sb("mixedT", [128, 8, 128], BF16)
                h2t = sb("h2t", [128, D], F32)
                CPOS = sb("CPOS", [128, NT, 8], F32)
                TOT = sb("TOT", [128, NT + 1, 8], F32)
                Bi = [sb("Bi%d" % k, [128, NT, 8], F32) for k in range(2)]

                ssq = small[:, 0:1]
                lnv = small[:, 1:2]
                rstd = small[:, 2:3]
                zz = small[:, 8:16]
                ee = small[:, 16:24]
                ll = small[:, 24:32]
                ssr = small[:, 32:36]
                lnr = small[:, 36:40]
                rstdr = small[:, 40:44]
                rden = small[:, 48:56]

                dma("sp", xt[0][:], xe[0:128, :], (), ["xt0"])
                dma("sp", g_attn[:], g_attn_d, (), ["g_attn"])
                dma("sp", cos_t[:], cos_d, (), ["cos"])
                dma("sp", sin_t[:], sin_d, (), ["sin"])
                dma("sp", D2[:], d2_d, (), ["D2"])
                dma("sp", Wq2[:], wq2_d, (), ["Wq2"])
                dma("sp", wk[:], wk_d, (), ["wk"])
                dma("sp", U[:], utri_d, (), ["U"])
                dma("sp", ones[:], ones_d, (), ["ones"])
                dma("sp", g_ret[:], g_ret_d, (), ["g_ret"])
                dma("sp", fb[:], fb_d, (), ["fb"])
                dma("sp", padneg[:], padneg_d, (), ["padneg"])
                dma("pool", mask[:], mask_d, (), ["mask"])
                w_in_v = w_in_d.rearrange("(kc p) n -> p kc n", p=128)
                for kc in range(8):
                    for (c0, c1) in ((0, 1540), (1540, INW)):
                        dma("pool", W_in[:, kc, c0:c1], w_in_v[:, kc, c0:c1], (), [("W_in", kc, c0)])
                w_out_v = w_out_d.rearrange("(kc p) n -> p kc n", p=128)
                for kc in range(8):
                    dma("pool", W_out[:, kc, :], w_out_v[:, kc, :], (), [("W_out", kc)])
                mset("dve", TOT[:, 0, :], 0.0, ["TOT"])
                mset("dve", R[:], 0.0, ["R"])
                mset("dve", R_bfz[:], 0.0, ["R_bfz"])
                mset("dve", rkTz[:], 0.0, ["rkTz"])
                mset("pool", VA[:], 1.0, ["VA"])
                chk(1)

                Sbank = [((S0, "ps:S0"), (S1, "ps:S1")), ((R0, "ps:R0"), (R1, "ps:R1"))]
                ff_ps = R1[:, 0:8]
                cs_ps = R1[:, 16:24]
                tot_ps = R1[:, 32:40]
                gctr = [0]

                for i in range(NT):
                    xb = i % 2
                    xk = "xt%d" % xb
                    x_t = xt[xb]
                    if i + 1 < NT:
                        dma("sp", xt[1 - xb][:], xe[(i + 1) * 128:(i + 2) * 128, :], (), ["xt%d" % (1 - xb)])
                    act(hn_bf[:], x_t[:], AF.Square, [xk], ["hn_bf", "ssq"], accum=ssq)
                    act(lnv, ssq, AF.Ln, ["ssq"], ["lnv"], scale=1.0 / D, bias=EPS)
                    act(rstd, lnv, AF.Exp, ["lnv"], ["rstd"], scale=-0.5)
                    stt(hn_bf[:], x_t[:], rstd, g_attn[:], ALU.mult, ALU.mult, [xk, "rstd", "g_attn"], ["hn_bf"])
                    for c in range(8):
                        tr(Ta[:, c, :], hn_bf[:, c * 128:(c + 1) * 128], ident[:], ["hn_bf", "ident"], ["ps:Ta"])
                    cp("dve", hnT[:], Ta[:], ["ps:Ta"], ["hnT"])
                    chk(2)

                    def proj(out_ap, n0, n1, okey):
                        for kc in range(8):
                            wkeys = [("W_in", kc, 0)] if n1 <= 1540 else ([("W_in", kc, 1540)] if n0 >= 1540 else
                                                                         [("W_in", kc, 0), ("W_in", kc, 1540)])
                            mm(out_ap, hnT[:, kc, :], W_in[:, kc, n0:n1], kc == 0, kc == 7, ["hnT"] + wkeys, [okey])

                    proj(ff_ps, 3072, 3080, "ps:R1")
                    tt("dve", zz, ff_ps, fb[:], ALU.add, ["ps:R1", "fb"], ["zz"])
                    act(ee, zz, AF.Exp, ["zz"], ["ee"], scale=-1.0)
                    act(ll, ee, AF.Ln, ["ee"], ["ll"], bias=1.0)
                    mm(cs_ps, U[:], ll, True, True, ["U", "ll"], ["ps:R1"])
                    mm(tot_ps, ones[:], ll, True, True, ["ones", "ll"], ["ps:R1"])
                    tt("dve", CPOS[:, i, :], cs_ps, TOT[:, i, :], ALU.add, ["ps:R1", "TOT"], ["CPOS"])
                    tt("dve", TOT[:, i + 1, :], tot_ps, TOT[:, i, :], ALU.add, ["ps:R1", "TOT"], ["TOT"])
                    bik = "Bi%d" % (i % 2)
                    tt("dve", Bi[i % 2][:, 0:i + 1, :], CPOS[:, 0:i + 1, :],
                       TOT[:, i + 1, :].unsqueeze(1).to_broadcast([128, i + 1, 8]), ALU.subtract,
                       ["CPOS", "TOT"], [bik])
                    if i > 0:
                        tt("dve", Bi[i % 2][:, 0, :], Bi[i % 2][:, 0, :], padneg[:], ALU.add, [bik, "padneg"], [bik])
                    chk(3)

                    proj(P0[:], 0, 512, "ps:P0")
                    cp("dve", qk[:], P0[:], ["ps:P0"], ["qk"])
                    proj(P1[:], 512, 1024, "ps:P1")
                    cp("act", rv_bf[:], P1[:], ["ps:P1"], ["rv_bf"])
                    proj(P0[:], 1024, 1536, "ps:P0")
                    act(eg[:], P0[:], AF.Exp, ["ps:P0"], ["eg"], scale=-1.0)
                    ts("dve", eg[:], eg[:], 1.0, None, ALU.add, None, ["eg"], ["eg"])
                    recip(eg[:], eg[:], ["eg"], ["eg"])
                    tt("dve", gate[:], P0[:], eg[:], ALU.mult, ["ps:P0", "eg"], ["gate"])
                    tt("pool", gate[:], gate[:], g_ret[:], ALU.mult, ["gate", "g_ret"], ["gate"])
                    proj(P1[:], 1536, 2048, "ps:P1")
                    cp("dve", fq_bf[:], P1[:], ["ps:P1"], ["fq_bf"])
                    proj(P0[:], 2048, 2560, "ps:P0")
                    cp("dve", fk_bf[:], P0[:], ["ps:P0"], ["fk_bf"])
                    proj(P1[:], 2560, 3072, "ps:P1")
                    cp("act", VA[:, i, :, 0:64], P1[:].rearrange("p (h d) -> p h d", h=8), ["ps:P1"], ["VA"])
                    chk(4)

                    qv = qk[:].rearrange("p (h t d) -> p h t d", h=8, t=2)
                    qrv = qkr[:].rearrange("p (h t d) -> p h t d", h=8, t=2)
                    x1 = qv[:, :, 0, :]
                    x2 = qv[:, :, 1, :]
                    cb_ = cos_t[:, i, :].unsqueeze(1).to_broadcast([128, 8, 32])
                    sb_ = sin_t[:, i, :].unsqueeze(1).to_broadcast([128, 8, 32])
                    t1v = t1[:].rearrange("p (h d) -> p h d", h=8)
                    t2v = t2[:].rearrange("p (h d) -> p h d", h=8)
                    tt("pool", t1v, x1, cb_, ALU.mult, ["qk", "cos"], ["t1"])
                    tt("pool", t2v, x2, sb_, ALU.mult, ["qk", "sin"], ["t2"])
                    tt("pool", qrv[:, :, 0, :], t1v, t2v, ALU.subtract, ["t1", "t2"], ["qkr"])
                    tt("pool", t1v, x1, sb_, ALU.mult, ["qk", "sin"], ["t1"])
                    tt("pool", t2v, x2, cb_, ALU.mult, ["qk", "cos"], ["t2"])
                    tt("pool", qrv[:, :, 1, :], t1v, t2v, ALU.add, ["t1", "t2"], ["qkr"])
                    tt("pool", kw_bf[:].rearrange("p (h d) -> p h d", h=4),
                       qkr[:, 256:512].rearrange("p (h d) -> p h d", h=4),
                       wk[:].unsqueeze(2).to_broadcast([128, 4, 64]), ALU.mult, ["qkr", "wk"], ["kw_bf"])
                    chk(5)

                    for c in range(4):
                        tr(Tb[:, c, :], qkr[:, c * 128:(c + 1) * 128], ident[:], ["qkr", "ident"], ["ps:Tb"])
                    cp("dve", rqT[:], Tb[:, 0:2, :], ["ps:Tb"], ["rqT"])
                    tt("dve", rqwT[:], Tb[:, 0:2, :], Wq2[:], ALU.mult, ["ps:Tb", "Wq2"], ["rqwT"])
                    rkTz4 = rkTz[:].rearrange("p (c t) n -> p c t n", t=2)
                    cp("dve", rkTz4[0:64, :, 0, :], Tb[0:64, 2:4, :], ["ps:Tb"], ["rkTz"])
                    cp("dve", rkTz4[64:128, :, 1, :], Tb[64:128, 2:4, :], ["ps:Tb"], ["rkTz"])
                    for c in range(4):
                        tr(Tb[:, 4 + c, :], fq_bf[:, c * 128:(c + 1) * 128], ident[:], ["fq_bf", "ident"], ["ps:Tb"])
                    cp("dve", fqT[:], Tb[:, 4:8, :], ["ps:Tb"], ["fqT"])
                    for c in range(4):
                        tr(Ta[:, c, :], fk_bf[:, c * 128:(c + 1) * 128], ident[:], ["fk_bf", "ident"], ["ps:Ta"])
                    cp("dve", FKT[:, :, i * 128:(i + 1) * 128], Ta[:, 0:4, :], ["ps:Ta"], [("FKT", i)])
                    chk(6)

                    R0v = R0[:].rearrange("p (h n) -> p h n", h=4)
                    R1v = R1[:].rearrange("p (h n) -> p h n", h=4)
                    R0u = R0[:].rearrange("p (c n) -> p c n", c=2)
                    for h in range(4):
                        c, hb = h // 2, 64 * (h % 2)
                        mm(R0v[:, h, :], rkTz[:, h, :], rqT[:, c, :], True, True, ["rkTz", "rqT"], ["ps:R0"])
                    tt("dve", STd[:], R0v, D2[:], ALU.mult, ["ps:R0", "D2"], ["STd"])
                    for h in range(4):
                        c, hb = h // 2, 64 * (h % 2)
                        mm(R1v[:, h, :], STd[:, h, :], rv_bf[:, h * 128:(h + 1) * 128], True, False, ["STd", "rv_bf"], ["ps:R1"])
                        mm(R1v[:, h, :], rqwT[:, c, :], R_bfz[:, h, :], False, True, ["rqwT", "R_bfz"], ["ps:R1"])
                    for c in range(2):
                        mm(R0u[:, c, :], kw_bf[:, c * 128:(c + 1) * 128], rv_bf[:, c * 256:(c + 1) * 256], True, True,
                           ["kw_bf", "rv_bf"], ["ps:R0"])
                    for h in range(4):
                        c, hb = h // 2, 64 * (h % 2)
                        stt(R[hb:hb + 64, c, :], R[hb:hb + 64, c, :], GD[h],
                            R0u[hb:hb + 64, c, (h % 2) * 128:(h % 2 + 1) * 128], ALU.mult, ALU.add, ["R", "ps:R0"], ["R"])
                    R_bfz4 = R_bfz[:].rearrange("p (c t) n -> p c t n", t=2)
                    cp("dve", R_bfz4[0:64, :, 0, :], R[0:64, :, :], ["R"], ["R_bfz"])
                    cp("dve", R_bfz4[64:128, :, 1, :], R[64:128, :, :], ["R"], ["R_bfz"])
                    chk(7)
                    for h in range(4):
                        act(junk[:], R1v[:, h, :], AF.Square, ["ps:R1"], ["junk", "ssr"], accum=ssr[:, h:h + 1])
                    act(lnr, ssr, AF.Ln, ["ssr"], ["lnr"], scale=1.0 / 128, bias=EPS)
                    act(rstdr, lnr, AF.Exp, ["lnr"], ["rstdr"], scale=-0.5)
                    for h in range(4):
                        stt(mixed[:, h * 128:(h + 1) * 128], R1v[:, h, :], rstdr[:, h:h + 1], gate[:, h * 128:(h + 1) * 128],
                            ALU.mult, ALU.mult, ["ps:R1", "rstdr", "gate"], ["mixed"])
                    chk(8)

                    ofA = P0[:, 0:260].rearrange("p (h d) -> p h d", h=4)
                    ofB = P1[:, 0:260].rearrange("p (h d) -> p h d", h=4)
                    pairs = [(c, j0) for c in range(4) for j0 in range(0, i + 1, 4)]
                    ginfo = {}

                    def emitS(g):
                        c, j0 = pairs[g]
                        gi = gctr[0] % 2
                        gctr[0] += 1
                        ginfo[g] = gi
                        banks = Sbank[gi]
                        nblk = min(4, i + 1 - j0)
                        for b in range(nblk):
                            j = j0 + b
                            for half in range(2):
                                bank, bk = banks[half]
                                hb = 64 * half
                                mm(bank[:, b * 128:(b + 1) * 128], FKT[hb:hb + 64, c, j * 128:(j + 1) * 128],
                                   fqT[hb:hb + 64, c, :], True, True, [("FKT", j), "fqT"], [bk])
                        for half in range(2):
                            bank, bk = banks[half]
                            h = 2 * c + half
                            for b in range(nblk):
                                j = j0 + b
                                pi = gi * 8 + half * 4 + b
                                pk = "PT%d" % pi
                                act(PT[pi][:], bank[:, b * 128:(b + 1) * 128], AF.Exp, [bk, bik], [pk],
                                    bias=Bi[i % 2][:, j, h:h + 1], scale=0.125)
                                if j == i:
                                    tt("pool", PT[pi][:], PT[pi][:], mask[:, (0 if i == 0 else 1), :], ALU.mult,
                                       [pk, "mask"], [pk])

                    def emitPV(g):
                        c, j0 = pairs[g]
                        gi = ginfo[g]
                        nblk = min(4, i + 1 - j0)
                        for half in range(2):
                            h = 2 * c + half
                            of = ofA if half == 0 else ofB
                            ok = "ps:P0" if half == 0 else "ps:P1"
                            for b in range(nblk):
                                j = j0 + b
                                pi = gi * 8 + half * 4 + b
                                mm(of[:, c, :], PT[pi][:], VA[:, j, h, :], j == 0, j == i, ["PT%d" % pi, "VA"], [ok])

                    ng = len(pairs)
                    for g in range(ng + 1):
                        if g < ng:
                            emitS(g)
                        if g >= 1:
                            emitPV(g - 1)
                    recip(rden[:, 0:4], ofA[:, :, 64], ["ps:P0"], ["rden"])
                    recip(rden[:, 4:8], ofB[:, :, 64], ["ps:P1"], ["rden"])
                    mfx = mixed[:, 512:1024].rearrange("p (c t d) -> p c t d", c=4, t=2)
                    tt("dve", mfx[:, :, 0, :], ofA[:, :, 0:64],
                       rden[:, 0:4].unsqueeze(2).to_broadcast([128, 4, 64]), ALU.mult, ["ps:P0", "rden"], ["mixed"])
                    tt("dve", mfx[:, :, 1, :], ofB[:, :, 0:64],
                       rden[:, 4:8].unsqueeze(2).to_broadcast([128, 4, 64]), ALU.mult, ["ps:P1", "rden"], ["mixed"])
                    chk(9)

                    for c in range(8):
                        tr(Ta[:, c, :], mixed[:, c * 128:(c + 1) * 128], ident[:], ["mixed", "ident"], ["ps:Ta"])
                    cp("act", mixedT[:], Ta[:], ["ps:Ta"], ["mixedT"])
                    for n, (pb, pk) in enumerate(((P0, "ps:P0"), (P1, "ps:P1"))):
                        for kc in range(8):
                            mm(pb[:], mixedT[:, kc, :], W_out[:, kc, n * 512:(n + 1) * 512], kc == 0, kc == 7,
                               ["mixedT", ("W_out", kc)], [pk])
                        tt("dve", h2t[:, n * 512:(n + 1) * 512], pb[:], x_t[:, n * 512:(n + 1) * 512], ALU.add,
                           [pk, xk], ["h2t"])
                    dma("sp", h2_d[i * 128:(i + 1) * 128, :], h2t[:], ["h2t"], [("h2d", i)])
                    chk(10)

                P.barrier()
                chk(11)

            with ExitStack() as e2:
                def sb2(name, shape, dt):
                    return e2.enter_context(nc.sbuf_tensor(name, list(shape), dt))

                W_up = sb2("W_up", [128, 8, 2 * DFF], BF16)
                W_dn = sb2("W_dn", [128, NFC, D], BF16)
                g_ffn = sb2("g_ffn", [128, D], F32)
                g_fin = sb2("g_fin", [128, D], F32)
                cw = sb2("cw", [128, NFC, 3], F32)
                cb = sb2("cb", [128, NFC], F32)
                h2a = [sb2("h2a%d" % k, [128, D], F32) for k in range(2)]
                h2r = [sb2("h2r%d" % k, [128, D], F32) for k in range(2)]
                hn2 = sb2("hn2", [128, D], BF16)
                junk2 = sb2("junk2", [128, D], BF16)
                hn2T = sb2("hn2T", [128, 8, 512], BF16)
                gT = sb2("gT", [128, NFC, 512], BF16)
                a_sb = [sb2("a_sb%d" % k, [128, 514], F32) for k in range(2)]
                acc = [sb2("acc%d" % k, [128, 512], F32) for k in range(2)]
                halo = sb2("halo", [128, NFC, 2], F32)
                small2 = sb2("small2", [128, 16], F32)
                ssq2, ln2, rstd2 = small2[:, 0:1], small2[:, 1:2], small2[:, 2:3]
                ssq3, ln3, rstd3 = small2[:, 4:5], small2[:, 5:6], small2[:, 6:7]

                dma("sp", g_ffn[:], g_ffn_d, (), ["g_ffn"])
                dma("sp", g_fin[:], g_fin_d, (), ["g_fin"])
                dma("sp", cw[:], cw_d, (), ["cw"])
                dma("sp", cb[:], cb_d, (), ["cb"])
                w_up_v = w_up_d.rearrange("(kc p) n -> p kc n", p=128)
                for kc in range(8):
                    for c0 in range(0, 2 * DFF, 1408):
                        dma("pool", W_up[:, kc, c0:c0 + 1408], w_up_v[:, kc, c0:c0 + 1408], (), [("W_up", kc, c0)])
                w_dn_v = w_down_d.rearrange("(fc p) n -> p fc n", p=128)
                for fc in range(NFC):
                    dma("pool", W_dn[:, fc, :], w_dn_v[:, fc, :], (), [("W_dn", fc)])
                    chk(12)

                def wup_keys(kc, n0, n1):
                    return [("W_up", kc, c0) for c0 in range(0, 2 * DFF, 1408) if c0 < n1 and c0 + 1408 > n0]

                actr = [0]

                def norm_to_T(row0, dstT, col0):
                    k = actr[0] % 2
                    actr[0] += 1
                    hk = "h2a%d" % k
                    dma("sp", h2a[k][:], h2_d[row0:row0 + 128, :], [("h2d", row0 // 128)], [hk])
                    act(junk2[:], h2a[k][:], AF.Square, [hk], ["junk2", "ssq2"], accum=ssq2)
                    act(ln2, ssq2, AF.Ln, ["ssq2"], ["ln2"], scale=1.0 / D, bias=EPS)
                    act(rstd2, ln2, AF.Exp, ["ln2"], ["rstd2"], scale=-0.5)
                    stt(hn2[:], h2a[k][:], rstd2, g_ffn[:], ALU.mult, ALU.mult, [hk, "rstd2", "g_ffn"], ["hn2"])
                    Tx, tk = (Ta, "ps:Ta") if k == 0 else (Tb, "ps:Tb")
                    for c in range(8):
                        tr(Tx[:, c, :], hn2[:, c * 128:(c + 1) * 128], ident[:], ["hn2", "ident"], [tk])
                    cp("dve", dstT[:, :, col0:col0 + 128], Tx[:], [tk], ["hn2T"])

                norm_to_T(0, hn2T, 0)
                for fc in range(NFC):
                    for kc in range(8):
                        mm(P0[:, 2 * fc:2 * fc + 2], W_up[:, kc, fc * 128:(fc + 1) * 128], hn2T[:, kc, 126:128],
                           kc == 0, kc == 7, ["hn2T"] + wup_keys(kc, fc * 128, (fc + 1) * 128), ["ps:P0"])
                cp("dve", halo[:].rearrange("p f t -> p (f t)"), P0[:, 0:2 * NFC], ["ps:P0"], ["halo"])
                chk(13)

                rctr = [0]
                for s in range(NS):
                    row0 = 128 + 512 * s
                    for t in range(4):
                        norm_to_T(row0 + 128 * t, hn2T, t * 128)
                    for fc in range(NFC):
                        (pa, pak), (pb, pbk) = ((P0, "ps:P0"), (P1, "ps:P1")) if fc % 2 == 0 else ((R0, "ps:R0"), (R1, "ps:R1"))
                        for kc in range(8):
                            mm(pa[:], W_up[:, kc, fc * 128:(fc + 1) * 128], hn2T[:, kc, :], kc == 0, kc == 7,
                               ["hn2T"] + wup_keys(kc, fc * 128, (fc + 1) * 128), [pak])
                        for kc in range(8):
                            mm(pb[:], W_up[:, kc, DFF + fc * 128:DFF + (fc + 1) * 128], hn2T[:, kc, :], kc == 0, kc == 7,
                               ["hn2T"] + wup_keys(kc, DFF + fc * 128, DFF + (fc + 1) * 128), [pbk])
                        ab = fc % 2
                        asb = a_sb[ab]
                        ak = "a_sb%d" % ab
                        ck = "acc%d" % ab
                        cp("pool", asb[:, 0:2], halo[:, fc, :], ["halo"], [ak])
                        cp("act", asb[:, 2:514], pa[:], [pak], [ak])
                        cp("pool", halo[:, fc, :], asb[:, 512:514], [ak], ["halo"])
                        act(acc[ab][:], pa[:], AF.Identity, [pak, "cw", "cb"], [ck], scale=cw[:, fc, 2:3], bias=cb[:, fc:fc + 1])
                        stt(acc[ab][:], asb[:, 1:513], cw[:, fc, 1:2], acc[ab][:], ALU.mult, ALU.add, [ak, ck, "cw"], [ck])
                        stt(acc[ab][:], asb[:, 0:512], cw[:, fc, 0:1], acc[ab][:], ALU.mult, ALU.add, [ak, ck, "cw"], [ck])
                        act(acc[ab][:], acc[ab][:], AF.Silu, [ck], [ck])
                        tt("dve", gT[:, fc, :], acc[ab][:], pb[:], ALU.mult, [ck, pbk], [("gT", fc)])
                    for t in range(4):
                        rb = rctr[0] % 2
                        rctr[0] += 1
                        rk = "h2r%d" % rb
                        rowt = row0 + 128 * t
                        dma("sp", h2r[rb][:], h2_d[rowt:rowt + 128, :], [("h2d", rowt // 128)], [rk])
                        for n, (pb, pk) in enumerate(((S0, "ps:S0"), (S1, "ps:S1"))):
                            for fc in range(NFC):
                                mm(pb[:], gT[:, fc, t * 128:(t + 1) * 128], W_dn[:, fc, n * 512:(n + 1) * 512],
                                   fc == 0, fc == NFC - 1, [("gT", fc), ("W_dn", fc)], [pk])
                            tt("dve", h2r[rb][:, n * 512:(n + 1) * 512], pb[:], h2r[rb][:, n * 512:(n + 1) * 512], ALU.add,
                               [pk, rk], [rk])
                        act(junk2[:], h2r[rb][:], AF.Square, [rk], ["junk2", "ssq3"], accum=ssq3)
                        act(ln3, ssq3, AF.Ln, ["ssq3"], ["ln3"], scale=1.0 / D, bias=EPS)
                        act(rstd3, ln3, AF.Exp, ["ln3"], ["rstd3"], scale=-0.5)
                        stt(h2r[rb][:], h2r[rb][:], rstd3, g_fin[:], ALU.mult, ALU.mult, [rk, "rstd3", "g_fin"], [rk])
                        orow = rowt - 128
                        dma("sp", out_d[orow:orow + 128, :], h2r[rb][:], [rk], [("out", orow)])
                P.finish()
                blk = es.enter_context(nc.Block())
                P.emit(blk)
        except _Stop:
            pass
    return nc


def host_constants(NT):
    L = NT * 128
    pos = np.arange(L, dtype=np.float32)
    half = 32
    inv = (1.0 / (10000.0 ** (np.arange(half, dtype=np.float32) / half))).astype(np.float32)
    ang = pos[:, None] * inv[None, :]
    cos = np.cos(ang).astype(np.float32).reshape(NT, 128, 32).transpose(1, 0, 2)
    sin = np.sin(ang).astype(np.float32).reshape(NT, 128, 32).transpose(1, 0, 2)
    gam = 1.0 - np.exp2(-5.0 - np.arange(RET_H, dtype=np.float64))
    m = np.arange(128)[:, None]
    n = np.arange(128)[None, :]
    d2 = np.zeros((128, 4, 128), np.float64)
    for h in range(RET_H):
        same = (m // 64) == (n // 64)
        cross = (m < 64) & (n >= 64)
        val = np.where(same, gam[h] ** np.abs(n - m), np.where(cross, gam[h] ** (n - m).clip(0), 0.0))
        d2[:, h, :] = val / 8.0
    wq2 = np.zeros((128, 2, 128), np.float64)
    for c in range(2):
        for half_ in range(2):
            h = 2 * c + half_
            wq2[half_ * 64:(half_ + 1) * 64, c, :] = (gam[h] ** (np.arange(128) + 1.0))[None, :]
    wk = np.zeros((128, 4), np.float64)
    for h in range(RET_H):
        wk[:, h] = gam[h] ** (127.0 - np.arange(128)) / 8.0
    k = np.arange(128)[:, None]
    q = np.arange(128)[None, :]
    mask1 = (k <= q).astype(np.float32)
    mask0 = (((k <= q) & (k >= NPAD)) | ((k == q) & (k < NPAD))).astype(np.float32)
    mask = np.stack([mask0, mask1], axis=1)
    return dict(cos_c=np.ascontiguousarray(cos), sin_c=np.ascontiguousarray(sin), d2=d2.astype(np.float32),
                wq2=wq2.astype(np.float32), wk_c=wk.astype(np.float32), mask_c=np.ascontiguousarray(mask),
                ident=np.eye(128, dtype=np.float32), utri=(k <= q).astype(np.float32),
                ones_c=np.ones((128, 128), np.float32),
                padneg=np.where(np.arange(128)[:, None] < NPAD, -30000.0, 0.0).astype(np.float32).repeat(8, axis=1))


def make_in_maps(inputs, NT, nb):
    f = lambda a: np.ascontiguousarray(np.asarray(a, dtype=np.float32))
    x = f(inputs["x"])
    meta = f(inputs["meta_tokens"])
    consts = host_constants(NT)
    rep = lambda v, n: np.ascontiguousarray(np.broadcast_to(f(v).reshape(1, n), (128, n)))
    shared = dict(
        w_in=f(inputs["w_in"])[0], w_out=f(inputs["w_out"])[0], w_up=f(inputs["w_up"])[0], w_down=f(inputs["w_down"])[0],
        g_attn_c=rep(inputs["attn_norm_g"], D), g_ffn_c=rep(inputs["ffn_norm_g"], D), g_fin_c=rep(inputs["final_norm_g"], D),
        g_ret_c=rep(inputs["ret_norm_g"], 512), fb_c=rep(inputs["fox_forget_b"], 8),
        cw_c=np.ascontiguousarray(f(inputs["conv_w"])[0].reshape(3, NFC, 128).transpose(2, 1, 0)),
        cb_c=np.ascontiguousarray(f(inputs["conv_b"])[0].reshape(NFC, 128).transpose(1, 0)),
        **consts,
    )
    maps = []
    S = (NT - 1) * 128
    for b in range(nb):
        xe = np.zeros((NT * 128, D), np.float32)
        xe[NPAD:128] = meta
        xe[128:] = x[b, :S]
        m = dict(shared)
        m["xe"] = xe
        maps.append(m)
    return maps


def kernel(**inputs):
    NT = 33
    nc = build_program(NT)
    maps = make_in_maps(inputs, NT, NCORES)
    res = run_bass_kernel_spmd(nc, maps, core_ids=list(range(NCORES)))
    out = np.stack([np.asarray(r["out"], dtype=np.float32) for r in res.results], axis=0)
    return out
```

```python
import numpy as np
from contextlib import ExitStack
import concourse.bass as bass
import concourse.mybir as mybir
from concourse.bass_utils import run_bass_kernel_spmd

F32 = mybir.dt.float32
BF16 = mybir.dt.bfloat16
AF = mybir.ActivationFunctionType
ALU = mybir.AluOpType

D = 1024
NPAD = 112
NMETA = 16
RET_H = 4
FOX_H = 8
DFF = 2816
NFC = DFF // 128
INW = 3080
EPS = 1e-6
NCORES = 8


class _Op:
    __slots__ = ("eng", "fn", "deps", "sig", "sigval", "dma", "dsem", "dval", "eidx", "dn")

    def __init__(self, eng, fn, dma):
        self.eng = eng
        self.fn = fn
        self.deps = ()
        self.sig = False
        self.sigval = 0
        self.dma = dma
        self.dsem = None
        self.dval = 0
        self.eidx = 0
        self.dn = 0


class Prog:
    COMPUTE = ("pe", "act", "dve", "pool")
    QUEUES = ("sp", "pool", "act")
    NPOOL = 12
    WINDOW = 6

    def __init__(self, nc, es):
        self.nc = nc
        self.engs = {"pe": nc.tensor, "act": nc.scalar, "dve": nc.vector, "pool": nc.gpsimd, "sp": nc.sync}
        self.ops = []
        self.eng_ops = {e: [] for e in self.engs}
        self.last_writer = {}
        self.readers = {}
        self.sems = {e: es.enter_context(nc.semaphore("s_" + e)) for e in self.COMPUTE}
        self.dma_pools = {q: [es.enter_context(nc.semaphore("d_%s%d" % (q, k))) for k in range(self.NPOOL)]
                          for q in self.QUEUES}
        self.dma_count = {q: 0 for q in self.QUEUES}
        self.barrier_deps = {e: None for e in self.engs}
        self.all_dma = []
        self.stopped = False

    def add(self, eng, fn, reads=(), writes=(), dma=False, force=False):
        if self.stopped and not force:
            return None
        op = _Op(eng, fn, dma)
        idx = len(self.ops)
        lst = self.eng_ops[eng]
        op.eidx = len(lst)
        deps = set()
        bd = self.barrier_deps[eng]
        if bd is not None:
            deps.update(bd)
            self.barrier_deps[eng] = None
        for r in reads:
            w = self.last_writer.get(r)
            if w is not None:
                deps.add(w)
            if isinstance(r, str) and r.startswith("ps:"):
                for o in self.readers.get(r, ()):
                    if self.ops[o].eng != eng:
                        deps.add(o)
        for w_ in writes:
            w = self.last_writer.get(w_)
            if w is not None:
                deps.add(w)
            rs = self.readers.get(w_)
            if rs:
                deps.update(rs)
        keep = []
        for d in deps:
            p = self.ops[d]
            if p.dma:
                keep.append(d)
            elif p.eng != eng:
                p.sig = True
                keep.append(d)
            else:
                if eng == "pe":
                    continue
                if op.eidx - p.eidx <= self.WINDOW:
                    p.sig = True
                    keep.append(d)
        op.deps = keep
        for w_ in writes:
            self.last_writer[w_] = idx
            self.readers[w_] = []
        for r in reads:
            self.readers.setdefault(r, []).append(idx)
        if dma:
            op.dn = self.dma_count[eng]
            self.dma_count[eng] += 1
            self.all_dma.append(idx)
        self.ops.append(op)
        lst.append(idx)
        return idx

    def barrier(self):
        deps = []
        for e in self.COMPUTE:
            for idx in reversed(self.eng_ops[e]):
                if not self.ops[idx].dma:
                    self.ops[idx].sig = True
                    deps.append(idx)
                    break
        deps.extend(self.all_dma)
        for e in self.engs:
            self.barrier_deps[e] = list(deps)
        self.last_writer = {}
        self.readers = {}

    def finish(self):
        self.barrier()
        self.add("sp", None, force=True)

    def emit(self, block):
        cnt = {e: 0 for e in self.COMPUTE}
        for op in self.ops:
            if op.dma:
                pool = self.dma_pools[op.eng]
                op.dsem = pool[op.dn % self.NPOOL]
                op.dval = 16 * (op.dn // self.NPOOL + 1)
            elif op.sig:
                cnt[op.eng] += 1
                op.sigval = cnt[op.eng]
        ops = self.ops
        sems = self.sems
        prog = self

        def run_engine(ename, eng):
            waited = {}
            for idx in prog.eng_ops[ename]:
                op = ops[idx]
                waits = {}
                for d in op.deps:
                    p = ops[d]
                    if p.dma:
                        s, v = p.dsem, p.dval
                    else:
                        s, v = sems[p.eng], p.sigval
                    k = id(s)
                    if k not in waits or waits[k][1] < v:
                        waits[k] = (s, v)
                if op.dma and op.dn >= prog.NPOOL:
                    s, v = op.dsem, op.dval - 16
                    k = id(s)
                    if k not in waits or waits[k][1] < v:
                        waits[k] = (s, v)
                for k, (s, v) in waits.items():
                    if waited.get(k, 0) < v:
                        eng.wait_ge(s, v)
                        waited[k] = v
                if op.fn is None:
                    continue
                inst = op.fn(eng)
                if op.dma:
                    inst.then_inc(op.dsem, 16)
                elif op.sig:
                    inst.then_inc(sems[ename], 1)

        @block.tensor
        def _(e):
            run_engine("pe", e)

        @block.scalar
        def _(e):
            run_engine("act", e)

        @block.vector
        def _(e):
            run_engine("dve", e)

        @block.gpsimd
        def _(e):
            run_engine("pool", e)

        @block.sync
        def _(e):
            run_engine("sp", e)


class _Stop(Exception):
    pass


def build_program(NT=33, debug=False, stop=None):
    assert (NT - 1) % 4 == 0
    L = NT * 128
    NS = (NT - 1) // 4
    nc = bass.Bass("TRN2", target_bir_lowering=False)

    def din(name, shape, dt=F32):
        return nc.dram_tensor(name, list(shape), dt, kind="ExternalInput").ap()

    xe = din("xe", [L, D])
    w_in_d = din("w_in", [D, INW])
    w_out_d = din("w_out", [D, D])
    w_up_d = din("w_up", [D, 2 * DFF])
    w_down_d = din("w_down", [DFF, D])
    g_attn_d = din("g_attn_c", [128, D])
    g_ffn_d = din("g_ffn_c", [128, D])
    g_fin_d = din("g_fin_c", [128, D])
    g_ret_d = din("g_ret_c", [128, 512])
    fb_d = din("fb_c", [128, 8])
    cw_d = din("cw_c", [128, NFC, 3])
    cb_d = din("cb_c", [128, NFC])
    cos_d = din("cos_c", [128, NT, 32])
    sin_d = din("sin_c", [128, NT, 32])
    d2_d = din("d2", [128, 4, 128])
    wq2_d = din("wq2", [128, 2, 128])
    wk_d = din("wk_c", [128, 4])
    mask_d = din("mask_c", [128, 2, 128])
    ident_d = din("ident", [128, 128])
    utri_d = din("utri", [128, 128])
    ones_d = din("ones_c", [128, 128])
    padneg_d = din("padneg", [128, 8])
    out_d = nc.dram_tensor("out", [L - 128, D], F32, kind="ExternalOutput").ap()
    h2_d = nc.dram_tensor("h2s", [L, D], F32, kind=("ExternalOutput" if debug else "Internal")).ap()

    GD = [float((1.0 - 2.0 ** (-5.0 - h)) ** 128) for h in range(RET_H)]

    with ExitStack() as es:
        P = Prog(nc, es)

        def chk(k):
            if stop is not None and stop == k:
                P.stopped = True

        def mm(out, lhsT, rhs, start, stop, reads, writes):
            P.add("pe", lambda e: e.matmul(out, lhsT=lhsT, rhs=rhs, start=start, stop=stop), reads, writes)

        def tr(out, in_, ident, reads, writes):
            P.add("pe", lambda e: e.transpose(out=out, in_=in_, identity=ident), reads, writes)

        def act(out, in_, func, reads, writes, bias=None, scale=None, accum=None):
            kw = {}
            if bias is not None:
                kw["bias"] = bias
            if scale is not None:
                kw["scale"] = scale
            if accum is not None:
                kw["accum_out"] = accum
            P.add("act", lambda e: e.activation(out=out, in_=in_, func=func, **kw), reads, writes)

        def tt(eng, out, in0, in1, op, reads, writes):
            P.add(eng, lambda e: e.tensor_tensor(out=out, in0=in0, in1=in1, op=op), reads, writes)

        def ts(eng, out, in0, s1, s2, op0, op1, reads, writes):
            if op1 is None:
                P.add(eng, lambda e: e.tensor_scalar(out=out, in0=in0, scalar1=s1, scalar2=None, op0=op0), reads, writes)
            else:
                P.add(eng, lambda e: e.tensor_scalar(out=out, in0=in0, scalar1=s1, scalar2=s2, op0=op0, op1=op1),
                      reads, writes)

        def stt(out, in0, scalar, in1, op0, op1, reads, writes):
            P.add("dve", lambda e: e.scalar_tensor_tensor(out=out, in0=in0, scalar=scalar, in1=in1, op0=op0, op1=op1),
                  reads, writes)

        def cp(eng, out, in_, reads, writes):
            if eng == "act":
                P.add("act", lambda e: e.copy(out=out, in_=in_), reads, writes)
            else:
                P.add(eng, lambda e: e.tensor_copy(out=out, in_=in_), reads, writes)

        def recip(out, in_, reads, writes):
            P.add("dve", lambda e: e.reciprocal(out=out, in_=in_), reads, writes)

        def mset(eng, ap, val, writes):
            P.add(eng, lambda e: e.memset(ap, val), (), writes)

        def dma(q, out, in_, reads, writes, **kw):
            P.add(q, lambda e: e.dma_start(out=out, in_=in_, **kw), reads, writes, dma=True)

        Ta = es.enter_context(nc.psum_tensor("Ta", [128, 512], F32))
        Tb = es.enter_context(nc.psum_tensor("Tb", [128, 512], F32))
        TaB = Ta[:].bitcast(BF16).rearrange("p (c n) -> p c n", c=8)
        TbB = Tb[:].bitcast(BF16).rearrange("p (c n) -> p c n", c=8)
        P0 = es.enter_context(nc.psum_tensor("P0", [128, 512], F32))
        P1 = es.enter_context(nc.psum_tensor("P1", [128, 512], F32))
        R0 = es.enter_context(nc.psum_tensor("R0", [128, 512], F32))
        R1 = es.enter_context(nc.psum_tensor("R1", [128, 512], F32))
        S0 = es.enter_context(nc.psum_tensor("S0", [128, 512], F32))
        S1 = es.enter_context(nc.psum_tensor("S1", [128, 512], F32))

        ident = es.enter_context(nc.sbuf_tensor("ident_sb", [128, 128], BF16))
        dma("pool", ident[:], ident_d, (), ["ident"])

        try:
            with ExitStack() as e1:
                def sb(name, shape, dt):
                    return e1.enter_context(nc.sbuf_tensor(name, list(shape), dt))

                W_in = sb("W_in", [128, 8, INW], BF16)
                W_out = sb("W_out", [128, 8, D], BF16)
                FKT = sb("FKT", [128, 4, L], BF16)
                VA = sb("VA", [128, NT, 8, 65], BF16)
                cos_t = sb("cos_t", [128, NT, 32], F32)
                sin_t = sb("sin_t", [128, NT, 32], F32)
                D2 = sb("D2", [128, 4, 128], F32)
                Wq2 = sb("Wq2", [128, 2, 128], F32)
                wk = sb("wk", [128, 4], F32)
                mask = sb("mask", [128, 2, 128], BF16)
                U = sb("U", [128, 128], F32)
                ones = sb("ones", [128, 128], F32)
                g_attn = sb("g_attn", [128, D], F32)
                g_ret = sb("g_ret", [128, 512], F32)
                fb = sb("fb", [128, 8], F32)
                padneg = sb("padneg_sb", [128, 8], F32)
                xt = [sb("xt%d" % k, [128, D], F32) for k in range(2)]
                hn_bf = sb("hn_bf", [128, D], BF16)
                hnT = sb("hnT", [128, 8, 128], BF16)
                small = sb("small", [128, 64], F32)
                qk = sb("qk", [128, 512], F32)
                t1 = sb("t1", [128, 256], F32)
                t2 = sb("t2", [128, 256], F32)
                qkr = sb("qkr", [128, 512], BF16)
                kw_bf = sb("kw_bf", [128, 256], BF16)
                rv_bf = sb("rv_bf", [128, 512], BF16)
                eg = sb("eg", [128, 512], F32)
                gate = sb("gate", [128, 512], F32)
                fq_bf = sb("fq_bf", [128, 512], BF16)
                fk_bf = sb("fk_bf", [128, 512], BF16)
                rqT = sb("rqT", [128, 2, 128], BF16)
                rqwT = sb("rqwT", [128, 2, 128], BF16)
                rkTz = sb("rkTz", [128, 4, 128], BF16)
                fqT = sb("fqT", [128, 4, 128], BF16)
                STd = sb("STd", [128, 4, 128], BF16)
                R = sb("R", [128, 2, 128], F32)
                R_bfz = sb("R_bfz", [128, 4, 128], BF16)
                NPT = 24
                PT = [sb("PT%d" % k, [128, 128], BF16) for k in range(NPT)]
                junk = sb("junk", [128, 128], BF16)
                mixed = sb("mixed", [128, D], BF16)
                mixedT = sb("mixedT", [128, 8, 128], BF16)
                h2t = sb("h2t", [128, D], F32)
                CPOS = sb("CPOS", [128, NT, 8], F32)
                TOT = sb("TOT", [128, NT + 1, 8], F32)
                Bi = [sb("Bi%d" % k, [128, NT, 8], F32) for k in range(2)]

                ssq = small[:, 0:1]
                lnv = small[:, 1:2]
                rstd = small[:, 2:3]
                zz = small[:, 8:16]
                ee = small[:, 16:24]
                ll = small[:, 24:32]
                ssr = small[:, 32:36]
                lnr = small[:, 36:40]
                rstdr = small[:, 40:44]
                rden = small[:, 48:56]

                dma("sp", xt[0][:], xe[0:128, :], (), ["xt0"])
                dma("sp", g_attn[:], g_attn_d, (), ["g_attn"])
                dma("sp", cos_t[:], cos_d, (), ["cos"])
                dma("sp", sin_t[:], sin_d, (), ["sin"])
                dma("sp", D2[:], d2_d, (), ["D2"])
                dma("sp", Wq2[:], wq2_d, (), ["Wq2"])
                dma("sp", wk[:], wk_d, (), ["wk"])
                dma("sp", U[:], utri_d, (), ["U"])
                dma("sp", ones[:], ones_d, (), ["ones"])
                dma("sp", g_ret[:], g_ret_d, (), ["g_ret"])
                dma("sp", fb[:], fb_d, (), ["fb"])
                dma("sp", padneg[:], padneg_d, (), ["padneg"])
                dma("pool", mask[:], mask_d, (), ["mask"])
                w_in_v = w_in_d.rearrange("(kc p) n -> p kc n", p=128)
                for ch in (6, 0, 1, 2, 3, 4, 5):
                    c0, c1 = ch * 512, min((ch + 1) * 512, INW)
                    for kc in range(8):
                        dma("pool", W_in[:, kc, c0:c1], w_in_v[:, kc, c0:c1], (), [("W_in", kc, ch)])
                w_out_v = w_out_d.rearrange("(kc p) n -> p kc n", p=128)
                for kc in range(8):
                    dma("pool", W_out[:, kc, :], w_out_v[:, kc, :], (), [("W_out", kc)])
                mset("dve", TOT[:, 0, :], 0.0, ["TOT"])
                mset("dve", R[:], 0.0, ["R"])
                mset("dve", R_bfz[:], 0.0, ["R_bfz"])
                mset("dve", rkTz[:], 0.0, ["rkTz"])
                mset("pool", VA[:], 1.0, ["VA"])
                chk(1)

                Sbank = [((S0, "ps:S0"), (S1, "ps:S1")), ((R0, "ps:R0"), (R1, "ps:R1")), ((Ta, "ps:Ta"), (Tb, "ps:Tb"))]
                ff_ps = R1[:, 0:8]
                cs_ps = R1[:, 16:24]
                tot_ps = R1[:, 32:40]
                gctr = [0]

                def prologue(i):
                    xb = i % 2
                    xk = "xt%d" % xb
                    x_t = xt[xb]
                    act(hn_bf[:], x_t[:], AF.Square, [xk], ["hn_bf", "ssq"], accum=ssq)
                    act(lnv, ssq, AF.Ln, ["ssq"], ["lnv"], scale=1.0 / D, bias=EPS)
                    act(rstd, lnv, AF.Exp, ["lnv"], ["rstd"], scale=-0.5)
                    stt(hn_bf[:], x_t[:], rstd, g_attn[:], ALU.mult, ALU.mult, [xk, "rstd", "g_attn"], ["hn_bf"])
                    for c in range(8):
                        tr(TaB[:, c, :], hn_bf[:, c * 128:(c + 1) * 128], ident[:], ["hn_bf", "ident"], ["ps:Ta"])
                    cp("dve", hnT[:], TaB, ["ps:Ta"], ["hnT"])

                prologue(0)
                chk(2)
                for i in range(NT):
                    xb = i % 2
                    xk = "xt%d" % xb
                    x_t = xt[xb]
                    if i + 1 < NT:
                        dma("sp", xt[1 - xb][:], xe[(i + 1) * 128:(i + 2) * 128, :], (), ["xt%d" % (1 - xb)])

                    def proj(out_ap, n0, n1, okey):
                        for kc in range(8):
                            wkeys = [("W_in", kc, n0 // 512)]
                            mm(out_ap, hnT[:, kc, :], W_in[:, kc, n0:n1], kc == 0, kc == 7, ["hnT"] + wkeys, [okey])

                    proj(ff_ps, 3072, 3080, "ps:R1")
                    tt("dve", zz, ff_ps, fb[:], ALU.add, ["ps:R1", "fb"], ["zz"])
                    act(ee, zz, AF.Exp, ["zz"], ["ee"], scale=-1.0)
                    act(ll, ee, AF.Ln, ["ee"], ["ll"], bias=1.0)
                    mm(cs_ps, U[:], ll, True, True, ["U", "ll"], ["ps:R1"])
                    mm(tot_ps, ones[:], ll, True, True, ["ones", "ll"], ["ps:R1"])
                    tt("dve", CPOS[:, i, :], cs_ps, TOT[:, i, :], ALU.add, ["ps:R1", "TOT"], ["CPOS"])
                    tt("dve", TOT[:, i + 1, :], tot_ps, TOT[:, i, :], ALU.add, ["ps:R1", "TOT"], ["TOT"])
                    bik = "Bi%d" % (i % 2)
                    tt("dve", Bi[i % 2][:, 0:i + 1, :], CPOS[:, 0:i + 1, :],
                       TOT[:, i + 1, :].unsqueeze(1).to_broadcast([128, i + 1, 8]), ALU.subtract,
                       ["CPOS", "TOT"], [bik])
                    if i > 0:
                        tt("dve", Bi[i % 2][:, 0, :], Bi[i % 2][:, 0, :], padneg[:], ALU.add, [bik, "padneg"], [bik])
                    chk(3)

                    proj(P0[:], 0, 512, "ps:P0")
                    cp("dve", qk[:], P0[:], ["ps:P0"], ["qk"])
                    proj(P1[:], 512, 1024, "ps:P1")
                    cp("act", rv_bf[:], P1[:], ["ps:P1"], ["rv_bf"])
                    proj(S0[:], 1024, 1536, "ps:S0")
                    act(eg[:], S0[:], AF.Exp, ["ps:S0"], ["eg"], scale=-1.0)
                    ts("dve", eg[:], eg[:], 1.0, None, ALU.add, None, ["eg"], ["eg"])
                    recip(eg[:], eg[:], ["eg"], ["eg"])
                    tt("dve", gate[:], S0[:], eg[:], ALU.mult, ["ps:S0", "eg"], ["gate"])
                    tt("pool", gate[:], gate[:], g_ret[:], ALU.mult, ["gate", "g_ret"], ["gate"])
                    proj(S1[:], 1536, 2048, "ps:S1")
                    cp("dve", fq_bf[:], S1[:], ["ps:S1"], ["fq_bf"])
                    proj(P0[:], 2048, 2560, "ps:P0")
                    cp("dve", fk_bf[:], P0[:], ["ps:P0"], ["fk_bf"])
                    proj(P1[:], 2560, 3072, "ps:P1")
                    cp("act", VA[:, i, :, 0:64], P1[:].rearrange("p (h d) -> p h d", h=8), ["ps:P1"], ["VA"])
                    chk(4)

                    qv = qk[:].rearrange("p (h t d) -> p h t d", h=8, t=2)
                    qrv = qkr[:].rearrange("p (h t d) -> p h t d", h=8, t=2)
                    x1 = qv[:, :, 0, :]
                    x2 = qv[:, :, 1, :]
                    cb_ = cos_t[:, i, :].unsqueeze(1).to_broadcast([128, 8, 32])
                    sb_ = sin_t[:, i, :].unsqueeze(1).to_broadcast([128, 8, 32])
                    t1v = t1[:].rearrange("p (h d) -> p h d", h=8)
                    t2v = t2[:].rearrange("p (h d) -> p h d", h=8)
                    tt("pool", t1v, x1, cb_, ALU.mult, ["qk", "cos"], ["t1"])
                    tt("pool", t2v, x2, sb_, ALU.mult, ["qk", "sin"], ["t2"])
                    tt("pool", qrv[:, :, 0, :], t1v, t2v, ALU.subtract, ["t1", "t2"], ["qkr"])
                    tt("pool", t1v, x1, sb_, ALU.mult, ["qk", "sin"], ["t1"])
                    tt("pool", t2v, x2, cb_, ALU.mult, ["qk", "cos"], ["t2"])
                    tt("pool", qrv[:, :, 1, :], t1v, t2v, ALU.add, ["t1", "t2"], ["qkr"])
                    tt("pool", kw_bf[:].rearrange("p (h d) -> p h d", h=4),
                       qkr[:, 256:512].rearrange("p (h d) -> p h d", h=4),
                       wk[:].unsqueeze(2).to_broadcast([128, 4, 64]), ALU.mult, ["qkr", "wk"], ["kw_bf"])
                    chk(5)

                    for c in range(4):
                        tr(TbB[:, c, :], qkr[:, c * 128:(c + 1) * 128], ident[:], ["qkr", "ident"], ["ps:Tb"])
                    cp("dve", rqT[:], TbB[:, 0:2, :], ["ps:Tb"], ["rqT"])
                    tt("dve", rqwT[:], TbB[:, 0:2, :], Wq2[:], ALU.mult, ["ps:Tb", "Wq2"], ["rqwT"])
                    rkTz4 = rkTz[:].rearrange("p (c t) n -> p c t n", t=2)
                    cp("dve", rkTz4[0:64, :, 0, :], TbB[0:64, 2:4, :], ["ps:Tb"], ["rkTz"])
                    cp("dve", rkTz4[64:128, :, 1, :], TbB[64:128, 2:4, :], ["ps:Tb"], ["rkTz"])
                    for c in range(4):
                        tr(TbB[:, 4 + c, :], fq_bf[:, c * 128:(c + 1) * 128], ident[:], ["fq_bf", "ident"], ["ps:Tb"])
                    cp("dve", fqT[:], TbB[:, 4:8, :], ["ps:Tb"], ["fqT"])
                    for c in range(4):
                        tr(TaB[:, c, :], fk_bf[:, c * 128:(c + 1) * 128], ident[:], ["fk_bf", "ident"], ["ps:Ta"])
                    cp("dve", FKT[:, :, i * 128:(i + 1) * 128], TaB[:, 0:4, :], ["ps:Ta"], [("FKT", i)])
                    if i + 1 < NT:
                        prologue(i + 1)
                    chk(6)

                    R0v = R0[:].rearrange("p (h n) -> p h n", h=4)
                    R1v = R1[:].rearrange("p (h n) -> p h n", h=4)
                    R0u = R0[:].rearrange("p (c n) -> p c n", c=2)
                    for h in range(4):
                        c, hb = h // 2, 64 * (h % 2)
                        mm(R0v[:, h, :], rkTz[:, h, :], rqT[:, c, :], True, True, ["rkTz", "rqT"], ["ps:R0"])
                    tt("dve", STd[:], R0v, D2[:], ALU.mult, ["ps:R0", "D2"], ["STd"])
                    for h in range(4):
                        c, hb = h // 2, 64 * (h % 2)
                        mm(R1v[:, h, :], STd[:, h, :], rv_bf[:, h * 128:(h + 1) * 128], True, False, ["STd", "rv_bf"], ["ps:R1"])
                        mm(R1v[:, h, :], rqwT[:, c, :], R_bfz[:, h, :], False, True, ["rqwT", "R_bfz"], ["ps:R1"])
                    for c in range(2):
                        mm(R0u[:, c, :], kw_bf[:, c * 128:(c + 1) * 128], rv_bf[:, c * 256:(c + 1) * 256], True, True,
                           ["kw_bf", "rv_bf"], ["ps:R0"])
                    for h in range(4):
                        c, hb = h // 2, 64 * (h % 2)
                        stt(R[hb:hb + 64, c, :], R[hb:hb + 64, c, :], GD[h],
                            R0u[hb:hb + 64, c, (h % 2) * 128:(h % 2 + 1) * 128], ALU.mult, ALU.add, ["R", "ps:R0"], ["R"])
                    R_bfz4 = R_bfz[:].rearrange("p (c t) n -> p c t n", t=2)
                    cp("dve", R_bfz4[0:64, :, 0, :], R[0:64, :, :], ["R"], ["R_bfz"])
                    cp("dve", R_bfz4[64:128, :, 1, :], R[64:128, :, :], ["R"], ["R_bfz"])
                    chk(7)
                    for h in range(4):
                        act(junk[:], R1v[:, h, :], AF.Square, ["ps:R1"], ["junk", "ssr"], accum=ssr[:, h:h + 1])
                    act(lnr, ssr, AF.Ln, ["ssr"], ["lnr"], scale=1.0 / 128, bias=EPS)
                    act(rstdr, lnr, AF.Exp, ["lnr"], ["rstdr"], scale=-0.5)
                    for h in range(4):
                        stt(mixed[:, h * 128:(h + 1) * 128], R1v[:, h, :], rstdr[:, h:h + 1], gate[:, h * 128:(h + 1) * 128],
                            ALU.mult, ALU.mult, ["ps:R1", "rstdr", "gate"], ["mixed"])
                    chk(8)

                    ofA = P0[:, 0:260].rearrange("p (h d) -> p h d", h=4)
                    ofB = P1[:, 0:260].rearrange("p (h d) -> p h d", h=4)
                    pairs = [(c, j0) for c in range(4) for j0 in range(0, i + 1, 4)]
                    ginfo = {}

                    def emitS(g):
                        c, j0 = pairs[g]
                        gi = gctr[0] % 3
                        gctr[0] += 1
                        ginfo[g] = gi
                        banks = Sbank[gi]
                        nblk = min(4, i + 1 - j0)
                        for b in range(nblk):
                            j = j0 + b
                            for half in range(2):
                                bank, bk = banks[half]
                                hb = 64 * half
                                mm(bank[:, b * 128:(b + 1) * 128], FKT[hb:hb + 64, c, j * 128:(j + 1) * 128],
                                   fqT[hb:hb + 64, c, :], True, True, [("FKT", j), "fqT"], [bk])
                        for half in range(2):
                            bank, bk = banks[half]
                            h = 2 * c + half
                            for b in range(nblk):
                                j = j0 + b
                                pi = gi * 8 + half * 4 + b
                                pk = "PT%d" % pi
                                act(PT[pi][:], bank[:, b * 128:(b + 1) * 128], AF.Exp, [bk, bik], [pk],
                                    bias=Bi[i % 2][:, j, h:h + 1], scale=0.125)
                                if j == i:
                                    tt("pool", PT[pi][:], PT[pi][:], mask[:, (0 if i == 0 else 1), :], ALU.mult,
                                       [pk, "mask"], [pk])

                    def emitPV(g):
                        c, j0 = pairs[g]
                        gi = ginfo[g]
                        nblk = min(4, i + 1 - j0)
                        for half in range(2):
                            h = 2 * c + half
                            of = ofA if half == 0 else ofB
                            ok = "ps:P0" if half == 0 else "ps:P1"
                            for b in range(nblk):
                                j = j0 + b
                                pi = gi * 8 + half * 4 + b
                                mm(of[:, c, :], PT[pi][:], VA[:, j, h, :], j == 0, j == i, ["PT%d" % pi, "VA"], [ok])

                    ng = len(pairs)
                    for g in range(ng + 2):
                        if g < ng:
                            emitS(g)
                        if g >= 2:
                            emitPV(g - 2)
                    recip(rden[:, 0:4], ofA[:, :, 64], ["ps:P0"], ["rden"])
                    recip(rden[:, 4:8], ofB[:, :, 64], ["ps:P1"], ["rden"])
                    mfx = mixed[:, 512:1024].rearrange("p (c t d) -> p c t d", c=4, t=2)
                    tt("dve", mfx[:, :, 0, :], ofA[:, :, 0:64],
                       rden[:, 0:4].unsqueeze(2).to_broadcast([128, 4, 64]), ALU.mult, ["ps:P0", "rden"], ["mixed"])
                    tt("dve", mfx[:, :, 1, :], ofB[:, :, 0:64],
                       rden[:, 4:8].unsqueeze(2).to_broadcast([128, 4, 64]), ALU.mult, ["ps:P1", "rden"], ["mixed"])
                    chk(9)

                    for c in range(8):
                        tr(TaB[:, c, :], mixed[:, c * 128:(c + 1) * 128], ident[:], ["mixed", "ident"], ["ps:Ta"])
                    cp("act", mixedT[:], TaB, ["ps:Ta"], ["mixedT"])
                    for n, (pb, pk) in enumerate(((P0, "ps:P0"), (P1, "ps:P1"))):
                        for kc in range(8):
                            mm(pb[:], mixedT[:, kc, :], W_out[:, kc, n * 512:(n + 1) * 512], kc == 0, kc == 7,
                               ["mixedT", ("W_out", kc)], [pk])
                        tt("dve", h2t[:, n * 512:(n + 1) * 512], pb[:], x_t[:, n * 512:(n + 1) * 512], ALU.add,
                           [pk, xk], ["h2t"])
                    dma("sp", h2_d[i * 128:(i + 1) * 128, :], h2t[:], ["h2t"], [("h2d", i)])
                    chk(10)

                P.barrier()
                chk(11)

            with ExitStack() as e2:
                def sb2(name, shape, dt):
                    return e2.enter_context(nc.sbuf_tensor(name, list(shape), dt))

                W_up = sb2("W_up", [128, 8, 2 * DFF], BF16)
                W_dn = sb2("W_dn", [128, NFC, D], BF16)
                g_ffn = sb2("g_ffn", [128, D], F32)
                g_fin = sb2("g_fin", [128, D], F32)
                cw = sb2("cw", [128, NFC, 3], F32)
                cb = sb2("cb", [128, NFC], F32)
                h2a = [sb2("h2a%d" % k, [128, D], F32) for k in range(2)]
                h2r = [sb2("h2r%d" % k, [128, D], F32) for k in range(2)]
                hn2 = sb2("hn2", [128, D], BF16)
                junk2 = sb2("junk2", [128, D], BF16)
                hn2T = sb2("hn2T", [128, 8, 512], BF16)
                gT = sb2("gT", [128, NFC, 512], BF16)
                a_sb = [sb2("a_sb%d" % k, [128, 514], F32) for k in range(2)]
                acc = [sb2("acc%d" % k, [128, 512], F32) for k in range(2)]
                halo = sb2("halo", [128, NFC, 2], F32)
                small2 = sb2("small2", [128, 16], F32)
                ssq2, ln2, rstd2 = small2[:, 0:1], small2[:, 1:2], small2[:, 2:3]
                ssq3, ln3, rstd3 = small2[:, 4:5], small2[:, 5:6], small2[:, 6:7]

                dma("sp", g_ffn[:], g_ffn_d, (), ["g_ffn"])
                dma("sp", g_fin[:], g_fin_d, (), ["g_fin"])
                dma("sp", cw[:], cw_d, (), ["cw"])
                dma("sp", cb[:], cb_d, (), ["cb"])
                w_up_v = w_up_d.rearrange("(kc p) n -> p kc n", p=128)
                for c0 in (0, 2816, 1408, 4224):
                    for kc in range(8):
                        dma("pool", W_up[:, kc, c0:c0 + 1408], w_up_v[:, kc, c0:c0 + 1408], (), [("W_up", kc, c0)])
                w_dn_v = w_down_d.rearrange("(fc p) n -> p fc n", p=128)
                for fc in range(NFC):
                    dma("pool", W_dn[:, fc, :], w_dn_v[:, fc, :], (), [("W_dn", fc)])
                    chk(12)

                def wup_keys(kc, n0, n1):
                    return [("W_up", kc, c0) for c0 in range(0, 2 * DFF, 1408) if c0 < n1 and c0 + 1408 > n0]

                actr = [0]

                def norm_to_T(row0, dstT, col0):
                    k = actr[0] % 2
                    actr[0] += 1
                    hk = "h2a%d" % k
                    dma("sp", h2a[k][:], h2_d[row0:row0 + 128, :], [("h2d", row0 // 128)], [hk])
                    act(junk2[:], h2a[k][:], AF.Square, [hk], ["junk2", "ssq2"], accum=ssq2)
                    act(ln2, ssq2, AF.Ln, ["ssq2"], ["ln2"], scale=1.0 / D, bias=EPS)
                    act(rstd2, ln2, AF.Exp, ["ln2"], ["rstd2"], scale=-0.5)
                    stt(hn2[:], h2a[k][:], rstd2, g_ffn[:], ALU.mult, ALU.mult, [hk, "rstd2", "g_ffn"], ["hn2"])
                    Tx, tk = (TaB, "ps:Ta") if k == 0 else (TbB, "ps:Tb")
                    for c in range(8):
                        tr(Tx[:, c, :], hn2[:, c * 128:(c + 1) * 128], ident[:], ["hn2", "ident"], [tk])
                    cp("dve", dstT[:, :, col0:col0 + 128], Tx, [tk], ["hn2T"])

                norm_to_T(0, hn2T, 0)
                for fc in range(NFC):
                    for kc in range(8):
                        mm(P0[:, 2 * fc:2 * fc + 2], W_up[:, kc, fc * 128:(fc + 1) * 128], hn2T[:, kc, 126:128],
                           kc == 0, kc == 7, ["hn2T"] + wup_keys(kc, fc * 128, (fc + 1) * 128), ["ps:P0"])
                cp("dve", halo[:].rearrange("p f t -> p (f t)"), P0[:, 0:2 * NFC], ["ps:P0"], ["halo"])
                chk(13)

                rctr = [0]
                for s in range(NS):
                    row0 = 128 + 512 * s
                    for t in range(4):
                        norm_to_T(row0 + 128 * t, hn2T, t * 128)
                    for fc in range(NFC):
                        (pa, pak), (pb, pbk) = ((P0, "ps:P0"), (P1, "ps:P1")) if fc % 2 == 0 else ((R0, "ps:R0"), (R1, "ps:R1"))
                        for kc in range(8):
                            mm(pa[:], W_up[:, kc, fc * 128:(fc + 1) * 128], hn2T[:, kc, :], kc == 0, kc == 7,
                               ["hn2T"] + wup_keys(kc, fc * 128, (fc + 1) * 128), [pak])
                        for kc in range(8):
                            mm(pb[:], W_up[:, kc, DFF + fc * 128:DFF + (fc + 1) * 128], hn2T[:, kc, :], kc == 0, kc == 7,
                               ["hn2T"] + wup_keys(kc, DFF + fc * 128, DFF + (fc + 1) * 128), [pbk])
                        ab = fc % 2
                        asb = a_sb[ab]
                        ak = "a_sb%d" % ab
                        ck = "acc%d" % ab
                        cp("pool", asb[:, 0:2], halo[:, fc, :], ["halo"], [ak])
                        cp("act", asb[:, 2:514], pa[:], [pak], [ak])
                        cp("pool", halo[:, fc, :], asb[:, 512:514], [ak], ["halo"])
                        act(acc[ab][:], pa[:], AF.Identity, [pak, "cw", "cb"], [ck], scale=cw[:, fc, 2:3], bias=cb[:, fc:fc + 1])
                        stt(acc[ab][:], asb[:, 1:513], cw[:, fc, 1:2], acc[ab][:], ALU.mult, ALU.add, [ak, ck, "cw"], [ck])
                        stt(acc[ab][:], asb[:, 0:512], cw[:, fc, 0:1], acc[ab][:], ALU.mult, ALU.add, [ak, ck, "cw"], [ck])
                        act(acc[ab][:], acc[ab][:], AF.Silu, [ck], [ck])
                        tt("dve", gT[:, fc, :], acc[ab][:], pb[:], ALU.mult, [ck, pbk], [("gT", fc)])
                    for t in range(4):
                        rb = rctr[0] % 2
                        rctr[0] += 1
                        rk = "h2r%d" % rb
                        rowt = row0 + 128 * t
                        dma("sp", h2r[rb][:], h2_d[rowt:rowt + 128, :], [("h2d", rowt // 128)], [rk])
                        for n, (pb, pk) in enumerate(((S0, "ps:S0"), (S1, "ps:S1"))):
                            for fc in range(NFC):
                                mm(pb[:], gT[:, fc, t * 128:(t + 1) * 128], W_dn[:, fc, n * 512:(n + 1) * 512],
                                   fc == 0, fc == NFC - 1, [("gT", fc), ("W_dn", fc)], [pk])
                            tt("dve", h2r[rb][:, n * 512:(n + 1) * 512], pb[:], h2r[rb][:, n * 512:(n + 1) * 512], ALU.add,
                               [pk, rk], [rk])
                        act(junk2[:], h2r[rb][:], AF.Square, [rk], ["junk2", "ssq3"], accum=ssq3)
                        act(ln3, ssq3, AF.Ln, ["ssq3"], ["ln3"], scale=1.0 / D, bias=EPS)
                        act(rstd3, ln3, AF.Exp, ["ln3"], ["rstd3"], scale=-0.5)
                        stt(h2r[rb][:], h2r[rb][:], rstd3, g_fin[:], ALU.mult, ALU.mult, [rk, "rstd3", "g_fin"], [rk])
                        orow = rowt - 128
                        dma("sp", out_d[orow:orow + 128, :], h2r[rb][:], [rk], [("out", orow)])
                P.finish()
                blk = es.enter_context(nc.Block())
                P.emit(blk)
        except _Stop:
            pass
    return nc


def host_constants(NT):
    L = NT * 128
    pos = np.arange(L, dtype=np.float32)
    half = 32
    inv = (1.0 / (10000.0 ** (np.arange(half, dtype=np.float32) / half))).astype(np.float32)
    ang = pos[:, None] * inv[None, :]
    cos = np.cos(ang).astype(np.float32).reshape(NT, 128, 32).transpose(1, 0, 2)
    sin = np.sin(ang).astype(np.float32).reshape(NT, 128, 32).transpose(1, 0, 2)
    gam = 1.0 - np.exp2(-5.0 - np.arange(RET_H, dtype=np.float64))
    m = np.arange(128)[:, None]
    n = np.arange(128)[None, :]
    d2 = np.zeros((128, 4, 128), np.float64)
    for h in range(RET_H):
        same = (m // 64) == (n // 64)
        cross = (m < 64) & (n >= 64)
        val = np.where(same, gam[h] ** np.abs(n - m), np.where(cross, gam[h] ** (n - m).clip(0), 0.0))
        d2[:, h, :] = val / 8.0
    wq2 = np.zeros((128, 2, 128), np.float64)
    for c in range(2):
        for half_ in range(2):
            h = 2 * c + half_
            wq2[half_ * 64:(half_ + 1) * 64, c, :] = (gam[h] ** (np.arange(128) + 1.0))[None, :]
    wk = np.zeros((128, 4), np.float64)
    for h in range(RET_H):
        wk[:, h] = gam[h] ** (127.0 - np.arange(128)) / 8.0
    k = np.arange(128)[:, None]
    q = np.arange(128)[None, :]
    mask1 = (k <= q).astype(np.float32)
    mask0 = (((k <= q) & (k >= NPAD)) | ((k == q) & (k < NPAD))).astype(np.float32)
    mask = np.stack([mask0, mask1], axis=1)
    return dict(cos_c=np.ascontiguousarray(cos), sin_c=np.ascontiguousarray(sin), d2=d2.astype(np.float32),
                wq2=wq2.astype(np.float32), wk_c=wk.astype(np.float32), mask_c=np.ascontiguousarray(mask),
                ident=np.eye(128, dtype=np.float32), utri=(k <= q).astype(np.float32),
                ones_c=np.ones((128, 128), np.float32),
                padneg=np.where(np.arange(128)[:, None] < NPAD, -30000.0, 0.0).astype(np.float32).repeat(8, axis=1))


def make_in_maps(inputs, NT, nb):
    f = lambda a: np.ascontiguousarray(np.asarray(a, dtype=np.float32))
    x = f(inputs["x"])
    meta = f(inputs["meta_tokens"])
    consts = host_constants(NT)
    rep = lambda v, n: np.ascontiguousarray(np.broadcast_to(f(v).reshape(1, n), (128, n)))
    shared = dict(
        w_in=f(inputs["w_in"])[0], w_out=f(inputs["w_out"])[0], w_up=f(inputs["w_up"])[0], w_down=f(inputs["w_down"])[0],
        g_attn_c=rep(inputs["attn_norm_g"], D), g_ffn_c=rep(inputs["ffn_norm_g"], D), g_fin_c=rep(inputs["final_norm_g"], D),
        g_ret_c=rep(inputs["ret_norm_g"], 512), fb_c=rep(inputs["fox_forget_b"], 8),
        cw_c=np.ascontiguousarray(f(inputs["conv_w"])[0].reshape(3, NFC, 128).transpose(2, 1, 0)),
        cb_c=np.ascontiguousarray(f(inputs["conv_b"])[0].reshape(NFC, 128).transpose(1, 0)),
        **consts,
    )
    maps = []
    S = (NT - 1) * 128
    for b in range(nb):
        xe = np.zeros((NT * 128, D), np.float32)
        xe[NPAD:128] = meta
        xe[128:] = x[b, :S]
        m = dict(shared)
        m["xe"] = xe
        maps.append(m)
    return maps


def kernel(**inputs):
    NT = 33
    nc = build_program(NT)
    maps = make_in_maps(inputs, NT, NCORES)
    res = run_bass_kernel_spmd(nc, maps, core_ids=list(range(NCORES)))
    out = np.stack([np.asarray(r["out"], dtype=np.float32) for r in res.results], axis=0)
    return out
```

```python
import numpy as np
from contextlib import ExitStack
import concourse.bass as bass
import concourse.mybir as mybir
from concourse.bass_utils import run_bass_kernel_spmd

F32 = mybir.dt.float32
BF16 = mybir.dt.bfloat16
AF = mybir.ActivationFunctionType
ALU = mybir.AluOpType

D = 1024
NPAD = 112
NMETA = 16
RET_H = 4
FOX_H = 8
DFF = 2816
NFC = DFF // 128
INW = 3080
EPS = 1e-6
NCORES = 8


class _Op:
    __slots__ = ("eng", "fn", "deps", "sig", "sigval", "dma", "dsem", "dval", "eidx", "dn")

    def __init__(self, eng, fn, dma):
        self.eng = eng
        self.fn = fn
        self.deps = ()
        self.sig = False
        self.sigval = 0
        self.dma = dma
        self.dsem = None
        self.dval = 0
        self.eidx = 0
        self.dn = 0


class Prog:
    COMPUTE = ("pe", "act", "dve", "pool")
    QUEUES = ("sp", "pool", "act")
    NPOOL = 12
    WINDOW = 6

    def __init__(self, nc, es):
        self.nc = nc
        self.engs = {"pe": nc.tensor, "act": nc.scalar, "dve": nc.vector, "pool": nc.gpsimd, "sp": nc.sync}
        self.ops = []
        self.eng_ops = {e: [] for e in self.engs}
        self.last_writer = {}
        self.readers = {}
        self.sems = {e: es.enter_context(nc.semaphore("s_" + e)) for e in self.COMPUTE}
        self.dma_pools = {q: [es.enter_context(nc.semaphore("d_%s%d" % (q, k))) for k in range(self.NPOOL)]
                          for q in self.QUEUES}
        self.dma_count = {q: 0 for q in self.QUEUES}
        self.barrier_deps = {e: None for e in self.engs}
        self.all_dma = []
        self.stopped = False

    def add(self, eng, fn, reads=(), writes=(), dma=False, force=False):
        if self.stopped and not force:
            return None
        op = _Op(eng, fn, dma)
        idx = len(self.ops)
        lst = self.eng_ops[eng]
        op.eidx = len(lst)
        deps = set()
        bd = self.barrier_deps[eng]
        if bd is not None:
            deps.update(bd)
            self.barrier_deps[eng] = None
        for r in reads:
            w = self.last_writer.get(r)
            if w is not None:
                deps.add(w)
            if isinstance(r, str) and r.startswith("ps:"):
                for o in self.readers.get(r, ()):
                    if self.ops[o].eng != eng:
                        deps.add(o)
        for w_ in writes:
            w = self.last_writer.get(w_)
            if w is not None:
                deps.add(w)
            rs = self.readers.get(w_)
            if rs:
                deps.update(rs)
        keep = []
        for d in deps:
            p = self.ops[d]
            if p.dma:
                keep.append(d)
            elif p.eng != eng:
                p.sig = True
                keep.append(d)
            else:
                if eng == "pe":
                    continue
                if op.eidx - p.eidx <= self.WINDOW:
                    p.sig = True
                    keep.append(d)
        op.deps = keep
        for w_ in writes:
            self.last_writer[w_] = idx
            self.readers[w_] = []
        for r in reads:
            self.readers.setdefault(r, []).append(idx)
        if dma:
            op.dn = self.dma_count[eng]
            self.dma_count[eng] += 1
            self.all_dma.append(idx)
        self.ops.append(op)
        lst.append(idx)
        return idx

    def barrier(self):
        deps = []
        for e in self.COMPUTE:
            for idx in reversed(self.eng_ops[e]):
                if not self.ops[idx].dma:
                    self.ops[idx].sig = True
                    deps.append(idx)
                    break
        deps.extend(self.all_dma)
        for e in self.engs:
            self.barrier_deps[e] = list(deps)
        self.last_writer = {}
        self.readers = {}

    def finish(self):
        self.barrier()
        self.add("sp", None, force=True)

    def emit(self, block):
        cnt = {e: 0 for e in self.COMPUTE}
        for op in self.ops:
            if op.dma:
                pool = self.dma_pools[op.eng]
                op.dsem = pool[op.dn % self.NPOOL]
                op.dval = 16 * (op.dn // self.NPOOL + 1)
            elif op.sig:
                cnt[op.eng] += 1
                op.sigval = cnt[op.eng]
        ops = self.ops
        sems = self.sems
        prog = self

        def run_engine(ename, eng):
            waited = {}
            for idx in prog.eng_ops[ename]:
                op = ops[idx]
                waits = {}
                for d in op.deps:
                    p = ops[d]
                    if p.dma:
                        s, v = p.dsem, p.dval
                    else:
                        s, v = sems[p.eng], p.sigval
                    k = id(s)
                    if k not in waits or waits[k][1] < v:
                        waits[k] = (s, v)
                if op.dma and op.dn >= prog.NPOOL:
                    s, v = op.dsem, op.dval - 16
                    k = id(s)
                    if k not in waits or waits[k][1] < v:
                        waits[k] = (s, v)
                for k, (s, v) in waits.items():
                    if waited.get(k, 0) < v:
                        eng.wait_ge(s, v)
                        waited[k] = v
                if op.fn is None:
                    continue
                inst = op.fn(eng)
                if op.dma:
                    inst.then_inc(op.dsem, 16)
                elif op.sig:
                    inst.then_inc(sems[ename], 1)

        @block.tensor
        def _(e):
            run_engine("pe", e)

        @block.scalar
        def _(e):
            run_engine("act", e)

        @block.vector
        def _(e):
            run_engine("dve", e)

        @block.gpsimd
        def _(e):
            run_engine("pool", e)

        @block.sync
        def _(e):
            run_engine("sp", e)


class _Stop(Exception):
    pass


def build_program(NT=33, debug=False, stop=None):
    assert (NT - 1) % 4 == 0
    L = NT * 128
    NS = (NT - 1) // 4
    nc = bass.Bass("TRN2", target_bir_lowering=False)

    def din(name, shape, dt=F32):
        return nc.dram_tensor(name, list(shape), dt, kind="ExternalInput").ap()

    xe = din("xe", [L, D])
    w_in_d = din("w_in", [D, INW])
    w_out_d = din("w_out", [D, D])
    w_up_d = din("w_up", [D, 2 * DFF])
    w_down_d = din("w_down", [DFF, D])
    g_attn_d = din("g_attn_c", [128, D])
    g_ffn_d = din("g_ffn_c", [128, D])
    g_fin_d = din("g_fin_c", [128, D])
    g_ret_d = din("g_ret_c", [128, 512])
    fb_d = din("fb_c", [128, 8])
    cw_d = din("cw_c", [128, NFC, 3])
    cb_d = din("cb_c", [128, NFC])
    cos_d = din("cos_c", [128, NT, 32])
    sin_d = din("sin_c", [128, NT, 32])
    d2_d = din("d2", [128, 4, 128])
    wq2_d = din("wq2", [128, 2, 128])
    wk_d = din("wk_c", [128, 4])
    mask_d = din("mask_c", [128, 2, 128])
    ident_d = din("ident", [128, 128])
    utri_d = din("utri", [128, 128])
    ones_d = din("ones_c", [128, 128])
    padneg_d = din("padneg", [128, 8])
    out_d = nc.dram_tensor("out", [L - 128, D], F32, kind="ExternalOutput").ap()
    h2_d = nc.dram_tensor("h2s", [L, D], F32, kind=("ExternalOutput" if debug else "Internal")).ap()

    GD = [float((1.0 - 2.0 ** (-5.0 - h)) ** 128) for h in range(RET_H)]

    with ExitStack() as es:
        P = Prog(nc, es)

        def chk(k):
            if stop is not None and stop == k:
                P.stopped = True

        def mm(out, lhsT, rhs, start, stop, reads, writes, sgc=False):
            P.add("pe", lambda e: e.matmul(out, lhsT=lhsT, rhs=rhs, start=start, stop=stop, skip_group_check=sgc),
                  reads, writes)

        def tr(out, in_, ident, reads, writes):
            P.add("pe", lambda e: e.transpose(out=out, in_=in_, identity=ident), reads, writes)

        def act(out, in_, func, reads, writes, bias=None, scale=None, accum=None):
            kw = {}
            if bias is not None:
                kw["bias"] = bias
            if scale is not None:
                kw["scale"] = scale
            if accum is not None:
                kw["accum_out"] = accum
            P.add("act", lambda e: e.activation(out=out, in_=in_, func=func, **kw), reads, writes)

        def tt(eng, out, in0, in1, op, reads, writes):
            P.add(eng, lambda e: e.tensor_tensor(out=out, in0=in0, in1=in1, op=op), reads, writes)

        def ts(eng, out, in0, s1, s2, op0, op1, reads, writes):
            if op1 is None:
                P.add(eng, lambda e: e.tensor_scalar(out=out, in0=in0, scalar1=s1, scalar2=None, op0=op0), reads, writes)
            else:
                P.add(eng, lambda e: e.tensor_scalar(out=out, in0=in0, scalar1=s1, scalar2=s2, op0=op0, op1=op1),
                      reads, writes)

        def stt(out, in0, scalar, in1, op0, op1, reads, writes):
            P.add("dve", lambda e: e.scalar_tensor_tensor(out=out, in0=in0, scalar=scalar, in1=in1, op0=op0, op1=op1),
                  reads, writes)

        def cp(eng, out, in_, reads, writes):
            if eng == "act":
                P.add("act", lambda e: e.copy(out=out, in_=in_), reads, writes)
            else:
                P.add(eng, lambda e: e.tensor_copy(out=out, in_=in_), reads, writes)

        def recip(out, in_, reads, writes):
            P.add("dve", lambda e: e.reciprocal(out=out, in_=in_), reads, writes)

        def mset(eng, ap, val, writes):
            P.add(eng, lambda e: e.memset(ap, val), (), writes)

        def dma(q, out, in_, reads, writes, **kw):
            P.add(q, lambda e: e.dma_start(out=out, in_=in_, **kw), reads, writes, dma=True)

        Ta = es.enter_context(nc.psum_tensor("Ta", [128, 512], F32))
        Tb = es.enter_context(nc.psum_tensor("Tb", [128, 512], F32))
        TaB = Ta[:].bitcast(BF16).rearrange("p (c n) -> p c n", c=8)
        TbB = Tb[:].bitcast(BF16).rearrange("p (c n) -> p c n", c=8)
        P0 = es.enter_context(nc.psum_tensor("P0", [128, 512], F32))
        P1 = es.enter_context(nc.psum_tensor("P1", [128, 512], F32))
        R0 = es.enter_context(nc.psum_tensor("R0", [128, 512], F32))
        R1 = es.enter_context(nc.psum_tensor("R1", [128, 512], F32))
        S0 = es.enter_context(nc.psum_tensor("S0", [128, 512], F32))
        S1 = es.enter_context(nc.psum_tensor("S1", [128, 512], F32))

        ident = es.enter_context(nc.sbuf_tensor("ident_sb", [128, 128], BF16))
        dma("pool", ident[:], ident_d, (), ["ident"])

        try:
            with ExitStack() as e1:
                def sb(name, shape, dt):
                    return e1.enter_context(nc.sbuf_tensor(name, list(shape), dt))

                W_in = sb("W_in", [128, 8, INW], BF16)
                W_out = sb("W_out", [128, 8, D], BF16)
                FKT = sb("FKT", [128, 4, L], BF16)
                VA = sb("VA", [128, NT, 8, 65], BF16)
                cos_t = sb("cos_t", [128, NT, 32], F32)
                sin_t = sb("sin_t", [128, NT, 32], F32)
                D2 = sb("D2", [128, 4, 128], F32)
                Wq2 = sb("Wq2", [128, 2, 128], F32)
                wk = sb("wk", [128, 4], F32)
                mask = sb("mask", [128, 2, 128], BF16)
                U = sb("U", [128, 128], F32)
                ones = sb("ones", [128, 128], F32)
                g_attn = sb("g_attn", [128, D], F32)
                g_ret = sb("g_ret", [128, 512], F32)
                fb = sb("fb", [128, 8], F32)
                padneg = sb("padneg_sb", [128, 8], F32)
                xt = [sb("xt%d" % k, [128, D], F32) for k in range(2)]
                hn_bf = sb("hn_bf", [128, D], BF16)
                hnT = sb("hnT", [128, 8, 128], BF16)
                small = sb("small", [128, 64], F32)
                qk = sb("qk", [128, 512], F32)
                t1 = sb("t1", [128, 256], F32)
                t2 = sb("t2", [128, 256], F32)
                qkr = sb("qkr", [128, 512], BF16)
                kw_bf = sb("kw_bf", [128, 256], BF16)
                rv_bf = sb("rv_bf", [128, 512], BF16)
                eg = sb("eg", [128, 512], F32)
                gate = sb("gate", [128, 512], F32)
                fq_bf = sb("fq_bf", [128, 512], BF16)
                fk_bf = sb("fk_bf", [128, 512], BF16)
                rqT = sb("rqT", [128, 2, 128], BF16)
                rqwT = sb("rqwT", [128, 2, 128], BF16)
                rkTz = sb("rkTz", [128, 4, 128], BF16)
                fqT = sb("fqT", [128, 4, 128], BF16)
                STd = sb("STd", [128, 4, 128], BF16)
                R = sb("R", [128, 2, 128], F32)
                R_bfz = sb("R_bfz", [128, 4, 128], BF16)
                NPT = 6
                PT = [sb("PT%d" % k, [128, 512], BF16) for k in range(NPT)]
                VS = [sb("VS%d" % k, [128, 8, 65], BF16) for k in range(8)]
                junk = sb("junk", [128, 128], BF16)
                mixed = sb("mixed", [128, D], BF16)
                mixedT = sb("mixedT", [128, 8, 128], BF16)
                CPOS = sb("CPOS", [128, NT, 8], F32)
                TOT = sb("TOT", [128, NT + 1, 8], F32)
                Bi = [sb("Bi%d" % k, [128, NT, 8], F32) for k in range(2)]

                ssq = small[:, 0:1]
                lnv = small[:, 1:2]
                rstd = small[:, 2:3]
                zz = small[:, 8:16]
                ee = small[:, 16:24]
                ll = small[:, 24:32]
                ssr = small[:, 32:36]
                lnr = small[:, 36:40]
                rstdr = small[:, 40:44]
                rden = small[:, 48:56]

                dma("sp", xt[0][:], xe[0:128, :], (), ["xt0"])
                dma("sp", g_attn[:], g_attn_d, (), ["g_attn"])
                dma("sp", cos_t[:], cos_d, (), ["cos"])
                dma("sp", sin_t[:], sin_d, (), ["sin"])
                dma("sp", D2[:], d2_d, (), ["D2"])
                dma("sp", Wq2[:], wq2_d, (), ["Wq2"])
                dma("sp", wk[:], wk_d, (), ["wk"])
                dma("sp", U[:], utri_d, (), ["U"])
                dma("sp", ones[:], ones_d, (), ["ones"])
                dma("sp", g_ret[:], g_ret_d, (), ["g_ret"])
                dma("sp", fb[:], fb_d, (), ["fb"])
                dma("sp", padneg[:], padneg_d, (), ["padneg"])
                dma("pool", mask[:], mask_d, (), ["mask"])
                w_in_v = w_in_d.rearrange("(kc p) n -> p kc n", p=128)
                for ch in (6, 0, 1, 2, 3, 4, 5):
                    c0, c1 = ch * 512, min((ch + 1) * 512, INW)
                    for kc in range(8):
                        dma("pool", W_in[:, kc, c0:c1], w_in_v[:, kc, c0:c1], (), [("W_in", kc, ch)])
                w_out_v = w_out_d.rearrange("(kc p) n -> p kc n", p=128)
                for kc in range(8):
                    dma("pool", W_out[:, kc, :], w_out_v[:, kc, :], (), [("W_out", kc)])
                mset("dve", TOT[:, 0, :], 0.0, ["TOT"])
                mset("dve", R[:], 0.0, ["R"])
                mset("dve", R_bfz[:], 0.0, ["R_bfz"])
                mset("dve", rkTz[:], 0.0, ["rkTz"])
                mset("pool", VA[:], 1.0, ["VA"])
                chk(1)

                Sbank = [((S0, "ps:S0"), (S1, "ps:S1")), ((R0, "ps:R0"), (R1, "ps:R1")), ((Ta, "ps:Ta"), (Tb, "ps:Tb"))]
                ff_ps = R1[:, 0:8]
                cs_ps = R1[:, 16:24]
                tot_ps = R1[:, 32:40]
                gctr = [0]
                vctr = [0]

                def prologue(i):
                    xb = i % 2
                    xk = "xt%d" % xb
                    x_t = xt[xb]
                    act(hn_bf[:], x_t[:], AF.Square, [xk], ["hn_bf", "ssq"], accum=ssq)
                    act(lnv, ssq, AF.Ln, ["ssq"], ["lnv"], scale=1.0 / D, bias=EPS)
                    act(rstd, lnv, AF.Exp, ["lnv"], ["rstd"], scale=-0.5)
                    stt(hn_bf[:], x_t[:], rstd, g_attn[:], ALU.mult, ALU.mult, [xk, "rstd", "g_attn"], ["hn_bf"])
                    for c in range(8):
                        tr(TaB[:, c, :], hn_bf[:, c * 128:(c + 1) * 128], ident[:], ["hn_bf", "ident"], ["ps:Ta"])
                    cp("dve", hnT[:], TaB, ["ps:Ta"], ["hnT"])

                prologue(0)
                chk(2)
                for i in range(NT):
                    xb = i % 2
                    xk = "xt%d" % xb
                    x_t = xt[xb]
                    if i + 1 < NT:
                        dma("sp", xt[1 - xb][:], xe[(i + 1) * 128:(i + 2) * 128, :], (), ["xt%d" % (1 - xb)])

                    def proj(out_ap, n0, n1, okey):
                        for kc in range(8):
                            wkeys = [("W_in", kc, n0 // 512)]
                            mm(out_ap, hnT[:, kc, :], W_in[:, kc, n0:n1], kc == 0, kc == 7, ["hnT"] + wkeys, [okey])

                    proj(ff_ps, 3072, 3080, "ps:R1")
                    tt("dve", zz, ff_ps, fb[:], ALU.add, ["ps:R1", "fb"], ["zz"])
                    act(ee, zz, AF.Exp, ["zz"], ["ee"], scale=-1.0)
                    act(ll, ee, AF.Ln, ["ee"], ["ll"], bias=1.0)
                    mm(cs_ps, U[:], ll, True, True, ["U", "ll"], ["ps:R1"])
                    mm(tot_ps, ones[:], ll, True, True, ["ones", "ll"], ["ps:R1"])
                    tt("dve", CPOS[:, i, :], cs_ps, TOT[:, i, :], ALU.add, ["ps:R1", "TOT"], ["CPOS"])
                    tt("dve", TOT[:, i + 1, :], tot_ps, TOT[:, i, :], ALU.add, ["ps:R1", "TOT"], ["TOT"])
                    bik = "Bi%d" % (i % 2)
                    tt("dve", Bi[i % 2][:, 0:i + 1, :], CPOS[:, 0:i + 1, :],
                       TOT[:, i + 1, :].unsqueeze(1).to_broadcast([128, i + 1, 8]), ALU.subtract,
                       ["CPOS", "TOT"], [bik])
                    if i > 0:
                        tt("dve", Bi[i % 2][:, 0, :], Bi[i % 2][:, 0, :], padneg[:], ALU.add, [bik, "padneg"], [bik])
                    act(Bi[i % 2][:, 0:i + 1, :], Bi[i % 2][:, 0:i + 1, :], AF.Exp, [bik], [bik])
                    chk(3)

                    proj(P0[:], 0, 512, "ps:P0")
                    cp("dve", qk[:], P0[:], ["ps:P0"], ["qk"])
                    proj(P1[:], 512, 1024, "ps:P1")
                    cp("act", rv_bf[:], P1[:], ["ps:P1"], ["rv_bf"])
                    proj(S0[:], 1024, 1536, "ps:S0")
                    act(eg[:], S0[:], AF.Exp, ["ps:S0"], ["eg"], scale=-1.0)
                    ts("dve", eg[:], eg[:], 1.0, None, ALU.add, None, ["eg"], ["eg"])
                    recip(eg[:], eg[:], ["eg"], ["eg"])
                    tt("dve", gate[:], S0[:], eg[:], ALU.mult, ["ps:S0", "eg"], ["gate"])
                    tt("pool", gate[:], gate[:], g_ret[:], ALU.mult, ["gate", "g_ret"], ["gate"])
                    proj(S1[:], 1536, 2048, "ps:S1")
                    cp("dve", fq_bf[:], S1[:], ["ps:S1"], ["fq_bf"])
                    proj(P0[:], 2048, 2560, "ps:P0")
                    cp("dve", fk_bf[:], P0[:], ["ps:P0"], ["fk_bf"])
                    proj(P1[:], 2560, 3072, "ps:P1")
                    cp("act", VA[:, i, :, 0:64], P1[:].rearrange("p (h d) -> p h d", h=8), ["ps:P1"], ["VA"])
                    chk(4)

                    qv = qk[:].rearrange("p (h t d) -> p h t d", h=8, t=2)
                    qrv = qkr[:].rearrange("p (h t d) -> p h t d", h=8, t=2)
                    x1 = qv[:, :, 0, :]
                    x2 = qv[:, :, 1, :]
                    cb_ = cos_t[:, i, :].unsqueeze(1).to_broadcast([128, 8, 32])
                    sb_ = sin_t[:, i, :].unsqueeze(1).to_broadcast([128, 8, 32])
                    t1v = t1[:].rearrange("p (h d) -> p h d", h=8)
                    t2v = t2[:].rearrange("p (h d) -> p h d", h=8)
                    tt("pool", t1v, x1, cb_, ALU.mult, ["qk", "cos"], ["t1"])
                    tt("pool", t2v, x2, sb_, ALU.mult, ["qk", "sin"], ["t2"])
                    tt("pool", qrv[:, :, 0, :], t1v, t2v, ALU.subtract, ["t1", "t2"], ["qkr"])
                    tt("pool", t1v, x1, sb_, ALU.mult, ["qk", "sin"], ["t1"])
                    tt("pool", t2v, x2, cb_, ALU.mult, ["qk", "cos"], ["t2"])
                    tt("pool", qrv[:, :, 1, :], t1v, t2v, ALU.add, ["t1", "t2"], ["qkr"])
                    tt("pool", kw_bf[:].rearrange("p (h d) -> p h d", h=4),
                       qkr[:, 256:512].rearrange("p (h d) -> p h d", h=4),
                       wk[:].unsqueeze(2).to_broadcast([128, 4, 64]), ALU.mult, ["qkr", "wk"], ["kw_bf"])
                    chk(5)

                    for c in range(4):
                        tr(TbB[:, c, :], qkr[:, c * 128:(c + 1) * 128], ident[:], ["qkr", "ident"], ["ps:Tb"])
                    cp("dve", rqT[:], TbB[:, 0:2, :], ["ps:Tb"], ["rqT"])
                    tt("dve", rqwT[:], TbB[:, 0:2, :], Wq2[:], ALU.mult, ["ps:Tb", "Wq2"], ["rqwT"])
                    rkTz4 = rkTz[:].rearrange("p (c t) n -> p c t n", t=2)
                    cp("dve", rkTz4[0:64, :, 0, :], TbB[0:64, 2:4, :], ["ps:Tb"], ["rkTz"])
                    cp("dve", rkTz4[64:128, :, 1, :], TbB[64:128, 2:4, :], ["ps:Tb"], ["rkTz"])
                    for c in range(4):
                        tr(TbB[:, 4 + c, :], fq_bf[:, c * 128:(c + 1) * 128], ident[:], ["fq_bf", "ident"], ["ps:Tb"])
                    cp("dve", fqT[:], TbB[:, 4:8, :], ["ps:Tb"], ["fqT"])
                    for c in range(4):
                        tr(TaB[:, c, :], fk_bf[:, c * 128:(c + 1) * 128], ident[:], ["fk_bf", "ident"], ["ps:Ta"])
                    cp("dve", FKT[:, :, i * 128:(i + 1) * 128], TaB[:, 0:4, :], ["ps:Ta"], [("FKT", i)])
                    if i + 1 < NT:
                        prologue(i + 1)
                    chk(6)

                    R0v = R0[:].rearrange("p (h n) -> p h n", h=4)
                    R1v = R1[:].rearrange("p (h n) -> p h n", h=4)
                    R0u = R0[:].rearrange("p (c n) -> p c n", c=2)
                    for h in range(4):
                        c, hb = h // 2, 64 * (h % 2)
                        mm(R0v[:, h, :], rkTz[:, h, :], rqT[:, c, :], True, True, ["rkTz", "rqT"], ["ps:R0"])
                    tt("dve", STd[:], R0v, D2[:], ALU.mult, ["ps:R0", "D2"], ["STd"])
                    for h in range(4):
                        c, hb = h // 2, 64 * (h % 2)
                        mm(R1v[:, h, :], STd[:, h, :], rv_bf[:, h * 128:(h + 1) * 128], True, False, ["STd", "rv_bf"], ["ps:R1"])
                        mm(R1v[:, h, :], rqwT[:, c, :], R_bfz[:, h, :], False, True, ["rqwT", "R_bfz"], ["ps:R1"])
                    for c in range(2):
                        mm(R0u[:, c, :], kw_bf[:, c * 128:(c + 1) * 128], rv_bf[:, c * 256:(c + 1) * 256], True, True,
                           ["kw_bf", "rv_bf"], ["ps:R0"])
                    for h in range(4):
                        c, hb = h // 2, 64 * (h % 2)
                        stt(R[hb:hb + 64, c, :], R[hb:hb + 64, c, :], GD[h],
                            R0u[hb:hb + 64, c, (h % 2) * 128:(h % 2 + 1) * 128], ALU.mult, ALU.add, ["R", "ps:R0"], ["R"])
                    R_bfz4 = R_bfz[:].rearrange("p (c t) n -> p c t n", t=2)
                    cp("dve", R_bfz4[0:64, :, 0, :], R[0:64, :, :], ["R"], ["R_bfz"])
                    cp("dve", R_bfz4[64:128, :, 1, :], R[64:128, :, :], ["R"], ["R_bfz"])
                    chk(7)
                    for h in range(4):
                        act(junk[:], R1v[:, h, :], AF.Square, ["ps:R1"], ["junk", "ssr"], accum=ssr[:, h:h + 1])
                    act(lnr, ssr, AF.Ln, ["ssr"], ["lnr"], scale=1.0 / 128, bias=EPS)
                    act(rstdr, lnr, AF.Exp, ["lnr"], ["rstdr"], scale=-0.5)
                    for h in range(4):
                        stt(mixed[:, h * 128:(h + 1) * 128], R1v[:, h, :], rstdr[:, h:h + 1], gate[:, h * 128:(h + 1) * 128],
                            ALU.mult, ALU.mult, ["ps:R1", "rstdr", "gate"], ["mixed"])
                    chk(8)

                    ofA = P0[:, 0:260].rearrange("p (h d) -> p h d", h=4)
                    ofB = P1[:, 0:260].rearrange("p (h d) -> p h d", h=4)
                    groups = [(j0, c) for j0 in range(0, i + 1, 4) for c in range(4)]
                    ginfo = {}
                    vsinfo = {}
                    first_in_bank = [True, True]

                    def emitS(g):
                        j0, c = groups[g]
                        gi = gctr[0] % 3
                        gctr[0] += 1
                        ginfo[g] = gi
                        banks = Sbank[gi]
                        nblk = min(4, i + 1 - j0)
                        if c == 0:
                            vset = vctr[0] % 2
                            vctr[0] += 1
                            vsinfo[j0] = vset
                            for b in range(nblk):
                                j = j0 + b
                                vk = "VS%d" % (vset * 4 + b)
                                tt("pool", VS[vset * 4 + b][:], VA[:, j, :, :],
                                   Bi[i % 2][:, j, :].unsqueeze(2).to_broadcast([128, 8, 65]), ALU.mult,
                                   ["VA", bik], [vk])
                        for b in range(nblk):
                            j = j0 + b
                            for half in range(2):
                                bank, bk = banks[half]
                                hb = 64 * half
                                mm(bank[:, b * 128:(b + 1) * 128], FKT[hb:hb + 64, c, j * 128:(j + 1) * 128],
                                   fqT[hb:hb + 64, c, :], True, True, [("FKT", j), "fqT"], [bk])
                        for half in range(2):
                            bank, bk = banks[half]
                            pi = gi * 2 + half
                            pk = "PT%d" % pi
                            act(PT[pi][:, 0:nblk * 128], bank[:, 0:nblk * 128], AF.Exp, [bk], [pk], scale=0.125)
                            if j0 + nblk - 1 == i:
                                bd = nblk - 1
                                tt("pool", PT[pi][:, bd * 128:(bd + 1) * 128], PT[pi][:, bd * 128:(bd + 1) * 128],
                                   mask[:, (0 if i == 0 else 1), :], ALU.mult, [pk, "mask"], [pk])

                    def emitPV(g):
                        j0, c = groups[g]
                        gi = ginfo[g]
                        vset = vsinfo[j0]
                        nblk = min(4, i + 1 - j0)
                        for half in range(2):
                            h = 2 * c + half
                            of = ofA if half == 0 else ofB
                            ok = "ps:P0" if half == 0 else "ps:P1"
                            pi = gi * 2 + half
                            for b in range(nblk):
                                j = j0 + b
                                st = first_in_bank[half]
                                first_in_bank[half] = False
                                mm(of[:, c, :], PT[pi][:, b * 128:(b + 1) * 128], VS[vset * 4 + b][:, h, :], st, j == i,
                                   ["PT%d" % pi, "VS%d" % (vset * 4 + b)], [ok], sgc=True)

                    ng = len(groups)
                    for g in range(ng + 2):
                        if g < ng:
                            emitS(g)
                        if g >= 2:
                            emitPV(g - 2)
                    recip(rden[:, 0:4], ofA[:, :, 64], ["ps:P0"], ["rden"])
                    recip(rden[:, 4:8], ofB[:, :, 64], ["ps:P1"], ["rden"])
                    mfx = mixed[:, 512:1024].rearrange("p (c t d) -> p c t d", c=4, t=2)
                    tt("dve", mfx[:, :, 0, :], ofA[:, :, 0:64],
                       rden[:, 0:4].unsqueeze(2).to_broadcast([128, 4, 64]), ALU.mult, ["ps:P0", "rden"], ["mixed"])
                    tt("dve", mfx[:, :, 1, :], ofB[:, :, 0:64],
                       rden[:, 4:8].unsqueeze(2).to_broadcast([128, 4, 64]), ALU.mult, ["ps:P1", "rden"], ["mixed"])
                    chk(9)

                    for c in range(8):
                        tr(TaB[:, c, :], mixed[:, c * 128:(c + 1) * 128], ident[:], ["mixed", "ident"], ["ps:Ta"])
                    cp("act", mixedT[:], TaB, ["ps:Ta"], ["mixedT"])
                    for n, (pb, pk) in enumerate(((P0, "ps:P0"), (P1, "ps:P1"))):
                        for kc in range(8):
                            mm(pb[:], mixedT[:, kc, :], W_out[:, kc, n * 512:(n + 1) * 512], kc == 0, kc == 7,
                               ["mixedT", ("W_out", kc)], [pk])
                        tt("dve", x_t[:, n * 512:(n + 1) * 512], pb[:], x_t[:, n * 512:(n + 1) * 512], ALU.add,
                           [pk, xk], [xk])
                    dma("sp", h2_d[i * 128:(i + 1) * 128, :], x_t[:], [xk], [("h2d", i)])
                    chk(10)

                P.barrier()
                chk(11)

            with ExitStack() as e2:
                def sb2(name, shape, dt):
                    return e2.enter_context(nc.sbuf_tensor(name, list(shape), dt))

                W_up = sb2("W_up", [128, 8, 2 * DFF], BF16)
                W_dn = sb2("W_dn", [128, NFC, D], BF16)
                g_ffn = sb2("g_ffn", [128, D], F32)
                g_fin = sb2("g_fin", [128, D], F32)
                cw = sb2("cw", [128, NFC, 3], F32)
                cb = sb2("cb", [128, NFC], F32)
                h2a = [sb2("h2a%d" % k, [128, D], F32) for k in range(2)]
                h2r = [sb2("h2r%d" % k, [128, D], F32) for k in range(2)]
                hn2 = sb2("hn2", [128, D], BF16)
                junk2 = sb2("junk2", [128, D], BF16)
                hn2T = sb2("hn2T", [128, 8, 512], BF16)
                gT = sb2("gT", [128, NFC, 512], BF16)
                a_sb = [sb2("a_sb%d" % k, [128, 514], F32) for k in range(2)]
                acc = [sb2("acc%d" % k, [128, 512], F32) for k in range(2)]
                halo = sb2("halo", [128, NFC, 2], F32)
                small2 = sb2("small2", [128, 16], F32)
                ssq2, ln2, rstd2 = small2[:, 0:1], small2[:, 1:2], small2[:, 2:3]
                ssq3, ln3, rstd3 = small2[:, 4:5], small2[:, 5:6], small2[:, 6:7]

                dma("sp", g_ffn[:], g_ffn_d, (), ["g_ffn"])
                dma("sp", g_fin[:], g_fin_d, (), ["g_fin"])
                dma("sp", cw[:], cw_d, (), ["cw"])
                dma("sp", cb[:], cb_d, (), ["cb"])
                w_up_v = w_up_d.rearrange("(kc p) n -> p kc n", p=128)
                for c0 in (0, 2816, 1408, 4224):
                    for kc in range(8):
                        dma("pool", W_up[:, kc, c0:c0 + 1408], w_up_v[:, kc, c0:c0 + 1408], (), [("W_up", kc, c0)])
                w_dn_v = w_down_d.rearrange("(fc p) n -> p fc n", p=128)
                for fc in range(NFC):
                    dma("pool", W_dn[:, fc, :], w_dn_v[:, fc, :], (), [("W_dn", fc)])
                    chk(12)

                def wup_keys(kc, n0, n1):
                    return [("W_up", kc, c0) for c0 in range(0, 2 * DFF, 1408) if c0 < n1 and c0 + 1408 > n0]

                actr = [0]

                def norm_to_T(row0, dstT, col0):
                    k = actr[0] % 2
                    actr[0] += 1
                    hk = "h2a%d" % k
                    dma("sp", h2a[k][:], h2_d[row0:row0 + 128, :], [("h2d", row0 // 128)], [hk])
                    act(junk2[:], h2a[k][:], AF.Square, [hk], ["junk2", "ssq2"], accum=ssq2)
                    act(ln2, ssq2, AF.Ln, ["ssq2"], ["ln2"], scale=1.0 / D, bias=EPS)
                    act(rstd2, ln2, AF.Exp, ["ln2"], ["rstd2"], scale=-0.5)
                    stt(hn2[:], h2a[k][:], rstd2, g_ffn[:], ALU.mult, ALU.mult, [hk, "rstd2", "g_ffn"], ["hn2"])
                    Tx, tk = (TaB, "ps:Ta") if k == 0 else (TbB, "ps:Tb")
                    for c in range(8):
                        tr(Tx[:, c, :], hn2[:, c * 128:(c + 1) * 128], ident[:], ["hn2", "ident"], [tk])
                    cp("dve", dstT[:, :, col0:col0 + 128], Tx, [tk], ["hn2T"])

                norm_to_T(0, hn2T, 0)
                for fc in range(NFC):
                    for kc in range(8):
                        mm(P0[:, 2 * fc:2 * fc + 2], W_up[:, kc, fc * 128:(fc + 1) * 128], hn2T[:, kc, 126:128],
                           kc == 0, kc == 7, ["hn2T"] + wup_keys(kc, fc * 128, (fc + 1) * 128), ["ps:P0"])
                cp("dve", halo[:].rearrange("p f t -> p (f t)"), P0[:, 0:2 * NFC], ["ps:P0"], ["halo"])
                chk(13)

                rctr = [0]
                for s in range(NS):
                    row0 = 128 + 512 * s
                    for t in range(4):
                        norm_to_T(row0 + 128 * t, hn2T, t * 128)
                    for fc in range(NFC):
                        (pa, pak), (pb, pbk) = ((P0, "ps:P0"), (P1, "ps:P1")) if fc % 2 == 0 else ((R0, "ps:R0"), (R1, "ps:R1"))
                        for kc in range(8):
                            mm(pa[:], W_up[:, kc, fc * 128:(fc + 1) * 128], hn2T[:, kc, :], kc == 0, kc == 7,
                               ["hn2T"] + wup_keys(kc, fc * 128, (fc + 1) * 128), [pak])
                        for kc in range(8):
                            mm(pb[:], W_up[:, kc, DFF + fc * 128:DFF + (fc + 1) * 128], hn2T[:, kc, :], kc == 0, kc == 7,
                               ["hn2T"] + wup_keys(kc, DFF + fc * 128, DFF + (fc + 1) * 128), [pbk])
                        ab = fc % 2
                        asb = a_sb[ab]
                        ak = "a_sb%d" % ab
                        ck = "acc%d" % ab
                        cp("pool", asb[:, 0:2], halo[:, fc, :], ["halo"], [ak])
                        cp("act", asb[:, 2:514], pa[:], [pak], [ak])
                        cp("pool", halo[:, fc, :], asb[:, 512:514], [ak], ["halo"])
                        act(acc[ab][:], pa[:], AF.Identity, [pak, "cw", "cb"], [ck], scale=cw[:, fc, 2:3], bias=cb[:, fc:fc + 1])
                        stt(acc[ab][:], asb[:, 1:513], cw[:, fc, 1:2], acc[ab][:], ALU.mult, ALU.add, [ak, ck, "cw"], [ck])
                        stt(acc[ab][:], asb[:, 0:512], cw[:, fc, 0:1], acc[ab][:], ALU.mult, ALU.add, [ak, ck, "cw"], [ck])
                        act(acc[ab][:], acc[ab][:], AF.Silu, [ck], [ck])
                        tt("dve", gT[:, fc, :], acc[ab][:], pb[:], ALU.mult, [ck, pbk], [("gT", fc)])
                    for t in range(4):
                        rb = rctr[0] % 2
                        rctr[0] += 1
                        rk = "h2r%d" % rb
                        rowt = row0 + 128 * t
                        dma("sp", h2r[rb][:], h2_d[rowt:rowt + 128, :], [("h2d", rowt // 128)], [rk])
                        for n, (pb, pk) in enumerate(((S0, "ps:S0"), (S1, "ps:S1"))):
                            for fc in range(NFC):
                                mm(pb[:], gT[:, fc, t * 128:(t + 1) * 128], W_dn[:, fc, n * 512:(n + 1) * 512],
                                   fc == 0, fc == NFC - 1, [("gT", fc), ("W_dn", fc)], [pk])
                            tt("dve", h2r[rb][:, n * 512:(n + 1) * 512], pb[:], h2r[rb][:, n * 512:(n + 1) * 512], ALU.add,
                               [pk, rk], [rk])
                        act(junk2[:], h2r[rb][:], AF.Square, [rk], ["junk2", "ssq3"], accum=ssq3)
                        act(ln3, ssq3, AF.Ln, ["ssq3"], ["ln3"], scale=1.0 / D, bias=EPS)
                        act(rstd3, ln3, AF.Exp, ["ln3"], ["rstd3"], scale=-0.5)
                        stt(h2r[rb][:], h2r[rb][:], rstd3, g_fin[:], ALU.mult, ALU.mult, [rk, "rstd3", "g_fin"], [rk])
                        orow = rowt - 128
                        dma("sp", out_d[orow:orow + 128, :], h2r[rb][:], [rk], [("out", orow)])
                P.finish()
                blk = es.enter_context(nc.Block())
                P.emit(blk)
        except _Stop:
            pass
    return nc


def host_constants(NT):
    L = NT * 128
    pos = np.arange(L, dtype=np.float32)
    half = 32
    inv = (1.0 / (10000.0 ** (np.arange(half, dtype=np.float32) / half))).astype(np.float32)
    ang = pos[:, None] * inv[None, :]
    cos = np.cos(ang).astype(np.float32).reshape(NT, 128, 32).transpose(1, 0, 2)
    sin = np.sin(ang).astype(np.float32).reshape(NT, 128, 32).transpose(1, 0, 2)
    gam = 1.0 - np.exp2(-5.0 - np.arange(RET_H, dtype=np.float64))
    m = np.arange(128)[:, None]
    n = np.arange(128)[None, :]
    d2 = np.zeros((128, 4, 128), np.float64)
    for h in range(RET_H):
        same = (m // 64) == (n // 64)
        cross = (m < 64) & (n >= 64)
        val = np.where(same, gam[h] ** np.abs(n - m), np.where(cross, gam[h] ** (n - m).clip(0), 0.0))
        d2[:, h, :] = val / 8.0
    wq2 = np.zeros((128, 2, 128), np.float64)
    for c in range(2):
        for half_ in range(2):
            h = 2 * c + half_
            wq2[half_ * 64:(half_ + 1) * 64, c, :] = (gam[h] ** (np.arange(128) + 1.0))[None, :]
    wk = np.zeros((128, 4), np.float64)
    for h in range(RET_H):
        wk[:, h] = gam[h] ** (127.0 - np.arange(128)) / 8.0
    k = np.arange(128)[:, None]
    q = np.arange(128)[None, :]
    mask1 = (k <= q).astype(np.float32)
    mask0 = (((k <= q) & (k >= NPAD)) | ((k == q) & (k < NPAD))).astype(np.float32)
    mask = np.stack([mask0, mask1], axis=1)
    return dict(cos_c=np.ascontiguousarray(cos), sin_c=np.ascontiguousarray(sin), d2=d2.astype(np.float32),
                wq2=wq2.astype(np.float32), wk_c=wk.astype(np.float32), mask_c=np.ascontiguousarray(mask),
                ident=np.eye(128, dtype=np.float32), utri=(k <= q).astype(np.float32),
                ones_c=np.ones((128, 128), np.float32),
                padneg=np.where(np.arange(128)[:, None] < NPAD, -30000.0, 0.0).astype(np.float32).repeat(8, axis=1))


def make_in_maps(inputs, NT, nb):
    f = lambda a: np.ascontiguousarray(np.asarray(a, dtype=np.float32))
    x = f(inputs["x"])
    meta = f(inputs["meta_tokens"])
    consts = host_constants(NT)
    rep = lambda v, n: np.ascontiguousarray(np.broadcast_to(f(v).reshape(1, n), (128, n)))
    shared = dict(
        w_in=f(inputs["w_in"])[0], w_out=f(inputs["w_out"])[0], w_up=f(inputs["w_up"])[0], w_down=f(inputs["w_down"])[0],
        g_attn_c=rep(inputs["attn_norm_g"], D), g_ffn_c=rep(inputs["ffn_norm_g"], D), g_fin_c=rep(inputs["final_norm_g"], D),
        g_ret_c=rep(inputs["ret_norm_g"], 512), fb_c=rep(inputs["fox_forget_b"], 8),
        cw_c=np.ascontiguousarray(f(inputs["conv_w"])[0].reshape(3, NFC, 128).transpose(2, 1, 0)),
        cb_c=np.ascontiguousarray(f(inputs["conv_b"])[0].reshape(NFC, 128).transpose(1, 0)),
        **consts,
    )
    maps = []
    S = (NT - 1) * 128
    for b in range(nb):
        xe = np.zeros((NT * 128, D), np.float32)
        xe[NPAD:128] = meta
        xe[128:] = x[b, :S]
        m = dict(shared)
        m["xe"] = xe
        maps.append(m)
    return maps


def kernel(**inputs):
    NT = 33
    nc = build_program(NT)
    maps = make_in_maps(inputs, NT, NCORES)
    res = run_bass_kernel_spmd(nc, maps, core_ids=list(range(NCORES)))
    out = np.stack([np.asarray(r["out"], dtype=np.float32) for r in res.results], axis=0)
    return out
```
